# Optimizing a Trainium2 kernel written in Bass

```python
import math
import jax, jax.numpy as jnp
from jax import lax
import numpy as np

D_MODEL = 1024
BATCH = 8
SEQ = 2048
DEPTH = 2

HEAD_DIM = 64
CONV_CH = D_MODEL // 4
LRU_WIDTH = D_MODEL // 4
LRU_HEADS = LRU_WIDTH // HEAD_DIM
ATT_WIDTH = D_MODEL // 2
ATT_HEADS = ATT_WIDTH // HEAD_DIM
D_MIX = CONV_CH + LRU_WIDTH + ATT_WIDTH
D_IN_PROJ = 2 * CONV_CH + 2 * LRU_WIDTH + 3 * ATT_WIDTH
IN_SPLITS = (CONV_CH, 2 * CONV_CH, 2 * CONV_CH + LRU_WIDTH, 2 * CONV_CH + 2 * LRU_WIDTH,
             2 * CONV_CH + 2 * LRU_WIDTH + ATT_WIDTH, 2 * CONV_CH + 2 * LRU_WIDTH + 2 * ATT_WIDTH)
CONV_KERNEL = 31
LRU_CONV_KERNEL = 4
LRU_C = 8.0
DILATED_PATTERNS = ((128, 1), (512, 4), (2048, 16))
ROPE_THETA = 10000.0
N_MEM = 256
MEM_HEADS = 4
MEM_HEAD_DIM = D_MODEL // MEM_HEADS
D_FF = 2816
FFN_CONV_KERNEL = 3
EPS = 1e-6

kernel_name = "hymba_style_conv_lru_dilated_attn_trunk"


def rms_norm(x, g):
    xf = x.astype(jnp.float32)
    y = xf * lax.rsqrt(jnp.mean(jnp.square(xf), -1, keepdims=True) + EPS)
    return (y * g.astype(jnp.float32)).astype(x.dtype)


def layer_norm(x, g, b):
    xf = x.astype(jnp.float32)
    mu = jnp.mean(xf, -1, keepdims=True)
    var = jnp.mean(jnp.square(xf - mu), -1, keepdims=True)
    y = (xf - mu) * lax.rsqrt(var + EPS) * g.astype(jnp.float32) + b.astype(jnp.float32)
    return y.astype(x.dtype)


def causal_dwconv(x, w, b):
    k = w.shape[0]
    y = lax.conv_general_dilated(x, w[:, None, :], window_strides=(1,), padding=((k - 1, 0),),
                                 dimension_numbers=('NWC', 'WIO', 'NWC'),
                                 feature_group_count=x.shape[-1])
    return y + b


def rope_tables(seq, dim):
    inv = 1.0 / (ROPE_THETA ** (jnp.arange(0, dim, 2, dtype=jnp.float32) / dim))
    ang = jnp.arange(seq, dtype=jnp.float32)[:, None] * inv[None, :]
    return jnp.cos(ang), jnp.sin(ang)


def apply_rope(x, cos, sin):
    x1, x2 = jnp.split(x.astype(jnp.float32), 2, axis=-1)
    c = cos[None, :, None, :]
    s = sin[None, :, None, :]
    return jnp.concatenate([x1 * c - x2 * s, x2 * c + x1 * s], axis=-1).astype(x.dtype)


def conformer_conv_mixer(a_val, a_gate, w, b, ln_g, ln_b):
    u = a_val * jax.nn.sigmoid(a_gate)
    u = causal_dwconv(u, w, b)
    u = layer_norm(u, ln_g, ln_b)
    return jax.nn.silu(u)


def rg_lru(x, w_a, b_a, w_i, b_i, lam):
    B, S, C = x.shape
    xh = x.reshape(B, S, LRU_HEADS, C // LRU_HEADS)
    r = jax.nn.sigmoid(jnp.einsum('bshi,hij->bshj', xh, w_a).reshape(B, S, C) + b_a)
    i = jax.nn.sigmoid(jnp.einsum('bshi,hij->bshj', xh, w_i).reshape(B, S, C) + b_i)
    log_a = -LRU_C * r.astype(jnp.float32) * jax.nn.softplus(-lam.astype(jnp.float32))
    a = jnp.exp(log_a)
    bterm = jnp.sqrt(-jnp.expm1(2.0 * log_a)) * (i * x).astype(jnp.float32)

    def combine(e1, e2):
        a1, b1 = e1
        a2, b2 = e2
        return a1 * a2, a2 * b1 + b2

    _, h = lax.associative_scan(combine, (a, bterm), axis=1)
    return h.astype(x.dtype)


def griffin_recurrent_mixer(b_x, b_gate, cw, cb, w_a, b_a, w_i, b_i, lam):
    u = causal_dwconv(b_x, cw, cb)
    h = rg_lru(u, w_a, b_a, w_i, b_i, lam)
    return h * jax.nn.gelu(b_gate)


def dilated_window_attn(q, k, v, window, dilation):
    B, S, H, Dh = q.shape
    n = window // dilation
    L = -(-S // dilation)
    Lp = -(-L // n) * n
    nb = Lp // n

    def to_blocks(t):
        t = jnp.pad(t, ((0, 0), (0, L * dilation - S), (0, 0), (0, 0)))
        t = t.reshape(B, L, dilation, H, Dh)
        t = jnp.pad(t, ((0, 0), (0, Lp - L), (0, 0), (0, 0), (0, 0)))
        return t.reshape(B, nb, n, dilation, H, Dh)

    def with_prev(t):
        prev = jnp.pad(t, ((0, 0), (1, 0), (0, 0), (0, 0), (0, 0), (0, 0)))[:, :-1]
        return jnp.concatenate([prev, t], axis=2)

    qb = to_blocks(q)
    kk = with_prev(to_blocks(k))
    vv = with_prev(to_blocks(v))
    s = jnp.einsum('bnqrhd,bnkrhd->bnrhqk', qb, kk).astype(jnp.float32) * (Dh ** -0.5)
    qi = jnp.arange(n)[:, None]
    ki = jnp.arange(2 * n)[None, :]
    rel = qi - ki + n
    blk_start = jnp.arange(nb)[:, None, None] * n
    mask = (rel >= 0) & (rel <= n) & (blk_start + ki - n >= 0)
    s = jnp.where(mask[None, :, None, None], s, -jnp.inf)
    m = jnp.max(s, axis=-1, keepdims=True)
    p = jnp.exp(s - m)
    den = jnp.sum(p, axis=-1, keepdims=True)
    o = jnp.einsum('bnrhqk,bnkrhd->bnqrhd', (p / den).astype(v.dtype), vv)
    lse = (m + jnp.log(den))[..., 0]
    o = o.reshape(B, Lp, dilation, H, Dh)[:, :L].reshape(B, L * dilation, H, Dh)[:, :S]
    lse = jnp.transpose(lse, (0, 1, 4, 2, 3)).reshape(B, Lp, dilation, H)[:, :L]
    lse = lse.reshape(B, L * dilation, H)[:, :S]
    return o.astype(jnp.float32), lse


def dilated_attention_mixer(q, k, v, cos, sin):
    B, S, _ = q.shape
    q = apply_rope(q.reshape(B, S, ATT_HEADS, HEAD_DIM), cos, sin)
    k = apply_rope(k.reshape(B, S, ATT_HEADS, HEAD_DIM), cos, sin)
    v = v.reshape(B, S, ATT_HEADS, HEAD_DIM)
    outs, lses = zip(*[dilated_window_attn(q, k, v, w, d) for (w, d) in DILATED_PATTERNS])
    wts = jax.nn.softmax(jnp.stack(lses, 0), axis=0)
    o = jnp.sum(wts[..., None] * jnp.stack(outs, 0), axis=0)
    return o.reshape(B, S, ATT_WIDTH).astype(q.dtype)


def memory_cross_attn(h, mem_n, w_q, w_kv, w_o):
    B, S, _ = h.shape
    M = mem_n.shape[1]
    q = (h @ w_q).reshape(B, S, MEM_HEADS, MEM_HEAD_DIM)
    k, v = jnp.split(mem_n @ w_kv, 2, axis=-1)
    k = k.reshape(B, M, MEM_HEADS, MEM_HEAD_DIM)
    v = v.reshape(B, M, MEM_HEADS, MEM_HEAD_DIM)
    s = jnp.einsum('bshd,bmhd->bhsm', q, k).astype(jnp.float32) * (MEM_HEAD_DIM ** -0.5)
    p = jax.nn.softmax(s, axis=-1).astype(v.dtype)
    o = jnp.einsum('bhsm,bmhd->bshd', p, v).reshape(B, S, D_MODEL)
    return o @ w_o


def setup_inputs(seed: int = 0) -> dict:
    key = jax.random.key(seed)
    ks = iter(jax.random.split(key, 40))
    L = DEPTH
    f32 = jnp.float32

    def nrm(shape, fan_in):
        return jax.random.normal(next(ks), shape, f32) * (fan_in ** -0.5)

    def gain(shape):
        return 1.0 + 0.05 * jax.random.normal(next(ks), shape, f32)

    def bias(shape):
        return 0.02 * jax.random.normal(next(ks), shape, f32)

    x = jax.random.normal(next(ks), (BATCH, SEQ, D_MODEL), f32)
    mem = jax.random.normal(next(ks), (BATCH, N_MEM, D_MODEL), f32)
    u = jax.random.uniform(next(ks), (L, LRU_WIDTH), f32, 0.9, 0.999)
    a0 = u ** (1.0 / LRU_C)
    lru_lambda = jnp.log(a0) - jnp.log1p(-a0)
    blk = LRU_WIDTH // LRU_HEADS
    return {
        "x": x,
        "mem": mem,
        "g_mix_pre": gain((L, D_MODEL)),
        "w_in": nrm((L, D_MODEL, D_IN_PROJ), D_MODEL),
        "conv_w": nrm((L, CONV_KERNEL, CONV_CH), CONV_KERNEL),
        "conv_b": bias((L, CONV_CH)),
        "conv_ln_g": gain((L, CONV_CH)),
        "conv_ln_b": bias((L, CONV_CH)),
        "lru_conv_w": nrm((L, LRU_CONV_KERNEL, LRU_WIDTH), LRU_CONV_KERNEL),
        "lru_conv_b": bias((L, LRU_WIDTH)),
        "lru_w_a": nrm((L, LRU_HEADS, blk, blk), blk),
        "lru_b_a": bias((L, LRU_WIDTH)),
        "lru_w_i": nrm((L, LRU_HEADS, blk, blk), blk),
        "lru_b_i": bias((L, LRU_WIDTH)),
        "lru_lambda": lru_lambda,
        "w_out": nrm((L, D_MIX, D_MODEL), D_MIX),
        "g_mix_post": gain((L, D_MODEL)),
        "g_mem_pre": gain((L, D_MODEL)),
        "g_mem_kv": gain((L, D_MODEL)),
        "w_mem_q": nrm((L, D_MODEL, D_MODEL), D_MODEL),
        "w_mem_kv": nrm((L, D_MODEL, 2 * D_MODEL), D_MODEL),
        "w_mem_o": nrm((L, D_MODEL, D_MODEL), D_MODEL),
        "g_mem_post": gain((L, D_MODEL)),
        "g_ffn_pre": gain((L, D_MODEL)),
        "w_up": nrm((L, D_MODEL, 2 * D_FF), D_MODEL),
        "ffn_conv_w": nrm((L, FFN_CONV_KERNEL, 2 * D_FF), FFN_CONV_KERNEL),
        "ffn_conv_b": bias((L, 2 * D_FF)),
        "w_down": nrm((L, D_FF, D_MODEL), D_FF),
        "g_ffn_post": gain((L, D_MODEL)),
    }


def reference(x, mem, g_mix_pre, w_in, conv_w, conv_b, conv_ln_g, conv_ln_b,
              lru_conv_w, lru_conv_b, lru_w_a, lru_b_a, lru_w_i, lru_b_i, lru_lambda,
              w_out, g_mix_post, g_mem_pre, g_mem_kv, w_mem_q, w_mem_kv, w_mem_o, g_mem_post,
              g_ffn_pre, w_up, ffn_conv_w, ffn_conv_b, w_down, g_ffn_post):
    cos, sin = rope_tables(x.shape[1], HEAD_DIM)
    for l in range(DEPTH):
        h = rms_norm(x, g_mix_pre[l])
        a_val, a_gate, b_x, b_gate, q, k, v = jnp.split(h @ w_in[l], IN_SPLITS, axis=-1)
        y_a = conformer_conv_mixer(a_val, a_gate, conv_w[l], conv_b[l], conv_ln_g[l], conv_ln_b[l])
        y_b = griffin_recurrent_mixer(b_x, b_gate, lru_conv_w[l], lru_conv_b[l], lru_w_a[l],
                                      lru_b_a[l], lru_w_i[l], lru_b_i[l], lru_lambda[l])
        y_c = dilated_attention_mixer(q, k, v, cos, sin)
        y = jnp.concatenate([y_a, y_b, y_c], axis=-1) @ w_out[l]
        x = x + rms_norm(y, g_mix_post[l])
        h = rms_norm(x, g_mem_pre[l])
        mem_n = rms_norm(mem, g_mem_kv[l])
        y = memory_cross_attn(h, mem_n, w_mem_q[l], w_mem_kv[l], w_mem_o[l])
        x = x + rms_norm(y, g_mem_post[l])
        h = rms_norm(x, g_ffn_pre[l])
        u = causal_dwconv(h @ w_up[l], ffn_conv_w[l], ffn_conv_b[l])
        gt, up = jnp.split(u, 2, axis=-1)
        y = (jax.nn.gelu(gt) * up) @ w_down[l]
        x = x + rms_norm(y, g_ffn_post[l])
    return x
```

```python
import numpy as np
import concourse.bass as bass
import concourse.mybir as mybir
from concourse.bass_utils import run_bass_kernel_spmd

F32 = mybir.dt.float32
BF16 = mybir.dt.bfloat16
AF = mybir.ActivationFunctionType
ALU = mybir.AluOpType

S = 2048
D = 1024
NL = 2
TB = 512
NTB = 4
NMEM = 256
DFF = 2816
NGC = 22
EPS = 1e-6
NCORES = 8

PC = {}
_o = 0
for _n, _w in [("g_mix_pre", 8), ("g_mix_post", 8), ("g_mem_pre", 8), ("g_mem_kv", 8), ("g_mem_post", 8),
               ("g_ffn_pre", 8), ("g_ffn_post", 8), ("conv_w", 62), ("conv_b", 2), ("conv_ln_g", 2),
               ("conv_ln_b", 2), ("lru_conv_w", 8), ("lru_conv_b", 2), ("lru_b_a", 2), ("lru_b_i", 2),
               ("lru_lambda", 2), ("ffn_conv_w", 132), ("ffn_conv_b", 44)]:
    PC[_n] = _o
    _o += _w
NPAR = _o
W_IN_EXT = 3584


class Op:
    __slots__ = ("stream", "fn", "deps", "dma", "chan", "val")

    def __init__(self, stream, fn, deps, dma, chan=None):
        self.stream, self.fn, self.deps, self.dma = stream, fn, deps, dma
        self.chan = (chan or ("dma_" + stream)) if dma else stream
        self.val = None


class Prog:
    def __init__(self):
        self.ops = []
        self.lw = {}
        self.rd = {}
        self.bar = None
        self.last = {}

    def add(self, stream, fn, r=(), w=(), dma=False, chan=None):
        i = len(self.ops)
        deps = set()
        if self.bar is not None:
            deps.add(self.bar)
        for k in r:
            j = self.lw.get(k)
            if j is not None:
                deps.add(j)
            if k.startswith("PS"):
                for j2 in self.rd.get(k, ()):
                    if self.ops[j2].stream != stream:
                        deps.add(j2)
        for k in w:
            j = self.lw.get(k)
            if j is not None:
                deps.add(j)
            deps.update(self.rd.get(k, ()))
        for k in r:
            self.rd.setdefault(k, []).append(i)
        for k in w:
            self.lw[k] = i
            self.rd[k] = []
        op = Op(stream, fn, deps, dma, chan)
        self.ops.append(op)
        self.last[op.chan] = i
        return i

    def barrier(self, dummy_ap):
        deps = set(self.last.values())
        i = len(self.ops)
        op = Op("pool", lambda e: e.memset(dummy_ap, 0.0), deps, False)
        self.ops.append(op)
        self.last[op.chan] = i
        self.bar = i
        self.lw = {}
        self.rd = {}

    def mm(self, out, lhsT, rhs, start, stop, r, w):
        self.add("pe", lambda e: e.matmul(out, lhsT, rhs, start=start, stop=stop), r, w)

    def act(self, out, in_, func, r, w, bias=None, scale=None):
        kw = {}
        if bias is not None:
            kw["bias"] = bias
        if scale is not None:
            kw["scale"] = scale
        self.add("act", lambda e: e.activation(out=out, in_=in_, func=func, **kw), r, w)

    def tt(self, eng, out, in0, in1, op, r, w):
        self.add(eng, lambda e: e.tensor_tensor(out=out, in0=in0, in1=in1, op=op), r, w)

    def ts(self, eng, out, in0, s1, s2, op0, op1, r, w):
        self.add(eng, lambda e: e.tensor_scalar(out=out, in0=in0, scalar1=s1, scalar2=s2, op0=op0, op1=op1), r, w)

    def stt(self, eng, out, in0, scalar, in1, op0, op1, r, w):
        self.add(eng, lambda e: e.scalar_tensor_tensor(out=out, in0=in0, scalar=scalar, in1=in1, op0=op0, op1=op1), r, w)

    def copy(self, eng, out, in_, r, w):
        if eng == "act":
            self.add("act", lambda e: e.copy(out=out, in_=in_), r, w)
        else:
            self.add(eng, lambda e: e.tensor_copy(out=out, in_=in_), r, w)

    def recip(self, out, in_, r, w):
        self.add("dve", lambda e: e.reciprocal(out=out, in_=in_), r, w)

    def scan(self, out, d0, d1, r, w):
        self.add("dve", lambda e: e.tensor_tensor_scan(out=out, data0=d0, data1=d1, initial=0.0,
                                                       op0=ALU.mult, op1=ALU.add), r, w)

    def memset(self, eng, ap, val, w):
        self.add(eng, lambda e: e.memset(ap, val), (), w)

    def dma(self, q, out, in_, r, w, chan):
        self.add(q, lambda e: e.dma_start(out=out, in_=in_), r, w, dma=True, chan="d_" + chan)

    def emit(self, nc, block):
        ops = self.ops
        needed = [False] * len(ops)

        def exempt(op, dj):
            return op.stream == "pe" and dj.stream == "pe" and not op.dma and not dj.dma

        for op in ops:
            for j in op.deps:
                if not exempt(op, ops[j]):
                    needed[j] = True
        cnt = {}
        for i, op in enumerate(ops):
            if op.dma or needed[i]:
                cnt[op.chan] = cnt.get(op.chan, 0) + (16 if op.dma else 1)
                op.val = cnt[op.chan]
        chans = sorted(cnt.keys())
        sems = {}
        import contextlib
        stack = contextlib.ExitStack()
        for c in chans:
            sems[c] = stack.enter_context(nc.semaphore("s_" + c))
        self._stack = stack

        def run_stream(stream, e):
            waited = {}
            for op in ops:
                if op.stream != stream:
                    continue
                reqs = {}
                for j in op.deps:
                    dj = ops[j]
                    if exempt(op, dj):
                        continue
                    if reqs.get(dj.chan, 0) < dj.val:
                        reqs[dj.chan] = dj.val
                for c, v in reqs.items():
                    if waited.get(c, 0) < v:
                        e.wait_ge(sems[c], v)
                        waited[c] = v
                inst = op.fn(e)
                if op.val is not None:
                    inst.then_inc(sems[op.chan], 16 if op.dma else 1)
            if stream == "sp":
                for c in chans:
                    if waited.get(c, 0) < cnt[c]:
                        e.wait_ge(sems[c], cnt[c])

        @block.sync
        def _(e):
            run_stream("sp", e)

        @block.tensor
        def _(e):
            run_stream("pe", e)

        @block.scalar
        def _(e):
            run_stream("act", e)

        @block.vector
        def _(e):
            run_stream("dve", e)

        @block.gpsimd
        def _(e):
            run_stream("pool", e)


def build(nl=NL, taps=()):
    nc = bass.Bass("TRN2", target_bir_lowering=False)
    dr = {}

    def din(name, shape):
        dr[name] = nc.dram_tensor(name, list(shape), F32, kind="ExternalInput").ap()
        return dr[name]

    xT = din("xT", [D, S])
    memT = din("memT", [D, NMEM])
    par = din("par", [NL, 128, NPAR])
    w_in = din("w_in", [NL, W_IN_EXT // 128, 128, 8, 128])
    gw = din("gw", [NL, 4, 128, 128])
    w_out = din("w_out", [NL, 8, 128, 8, 128])
    w_mq = din("w_mq", [NL, 8, 128, 8, 128])
    w_mkv = din("w_mkv", [NL, 16, 128, 8, 128])
    w_mo = din("w_mo", [NL, 8, 128, 8, 128])
    w_up = din("w_up", [NL, 2 * NGC, 128, 8, 128])
    w_dn = din("w_dn", [NL, 8, 128, NGC, 128])
    cos_d = din("cos", [128, S])
    sin_d = din("sin", [128, S])
    msk_d = din("msk", [128, 2 * TB])
    ident_d = din("ident", [128, 128])
    perm_d = din("perm", [128, 128])
    outT = nc.dram_tensor("outT", [D, S], F32, kind="ExternalOutput").ap()
    tap_out = {}
    for t in taps:
        tap_out[t] = nc.dram_tensor("tap_" + t, [D, S], F32, kind="ExternalOutput").ap()

    P = Prog()
    import contextlib
    es = contextlib.ExitStack()
    with es:
        NA = 52600
        arena = es.enter_context(nc.sbuf_tensor("arena", [128, NA], F32))
        pss = [es.enter_context(nc.psum_tensor("ps%d" % i, [128, TB], F32)) for i in range(8)]

        class Ar:
            off = 0

        def a_f32(off, n):
            assert off % 4 == 0
            return arena[:, off // 4: off // 4 + n]

        def a_bf(off, n):
            assert off % 4 == 0 and n % 2 == 0
            return arena[:, off // 4: off // 4 + n // 2].bitcast(BF16)

        def alloc(nbytes):
            o = Ar.off
            Ar.off += (nbytes + 3) // 4 * 4
            assert Ar.off <= NA * 4, (Ar.off, NA * 4)
            return o

        X = a_f32(alloc(8 * S * 4), 8 * S).rearrange("p (c t) -> p c t", c=8)
        H = a_bf(alloc(8 * S * 2), 8 * S).rearrange("p (c t) -> p c t", c=8)
        PAR = a_f32(alloc(NL * NPAR * 4), NL * NPAR).rearrange("p (l n) -> p l n", l=NL)
        MSK = a_bf(alloc(2 * TB * 2), 2 * TB).rearrange("p (a t) -> p a t", a=2)
        ONES = a_bf(alloc(128 * 2), 128)
        IDENT = a_bf(alloc(128 * 2), 128)
        PERMT = a_bf(alloc(128 * 2), 128)
        CST = a_f32(alloc(16 * 4), 16)
        SQ = a_bf(alloc(4 * TB * 2), 4 * TB).rearrange("p (a t) -> p a t", a=4)
        RS = a_f32(alloc(2 * TB * 4), 2 * TB).rearrange("p (a t) -> p a t", a=2)
        WS = a_bf(alloc(4 * 8 * 128 * 2), 4 * 8 * 128).rearrange("p (b k n) -> p b k n", b=4, k=8)
        CSNRAW = a_f32(alloc((2 * 2 * TB + 8) * 4), 2 * 2 * TB + 8)
        CSN = CSNRAW[:, 0:2 * 2 * TB].rearrange("p (b a t) -> p b a t", b=2, a=2)
        ZOFF = Ar.off
        ZSIZE = NA * 4 - ZOFF

        def z_f32(off, n):
            assert off + n * 4 <= ZSIZE, (off, n, ZSIZE)
            return a_f32(ZOFF + off, n)

        def z_bf(off, n):
            assert off + n * 2 <= ZSIZE, (off, n, ZSIZE)
            return a_bf(ZOFF + off, n)

        EPSAP = CST[:, 0:1]
        ONEAP = CST[:, 1:2]
        DUMMY = CST[:, 2:3]

        st = {"ps": 0, "sq": 0, "rs": 0, "ws": 0, "pt": 0, "csn": 0}

        def psum():
            i = st["ps"] % 8
            st["ps"] += 1
            return pss[i], "PS%d" % i

        def tbs(tb):
            return slice(tb * TB, (tb + 1) * TB)

        def barrier():
            P.barrier(DUMMY)

        P.memset("dve", CST[:, 0:1], EPS, ["CST"])
        P.memset("dve", CST[:, 1:2], 1.0, ["CST"])
        P.memset("dve", CST[:, 2:3], 0.0, ["CST"])
        P.memset("dve", ONES, 1.0, ["ONES"])
        for l in range(NL):
            P.dma("sp", PAR[:, l, :], par[l], [], ["PAR"], "setup")
        P.dma("pool", MSK.rearrange("p a t -> p (a t)"), msk_d, [], ["MSK"], "setup2")
        P.dma("pool", IDENT, ident_d, [], ["IDENT"], "setup2")
        P.dma("pool", PERMT, perm_d, [], ["PERMT"], "setup2")
        barrier()
        for c in range(8):
            P.dma("sp", X[:, c, :], xT[c * 128:(c + 1) * 128, :], [], ["X%d.%d" % (c, tb) for tb in range(NTB)], "x%d" % c)

        def XK(c, tb):
            return "X%d.%d" % (c, tb)

        def HK(c, tb):
            return "H%d.%d" % (c, tb)

        def pre_norm(l, gname, tb_list=None, dst=None):
            g0 = PC[gname]
            if tb_list is None:
                tb_list = range(NTB)
            if dst is None:
                dst = lambda c, tb: (H[:, c, tbs(tb)], HK(c, tb))
            for tb in tb_list:
                ps, pk = psum()
                for c in range(8):
                    si = st["sq"] % 4
                    st["sq"] += 1
                    P.act(SQ[:, si, :], X[:, c, tbs(tb)], AF.Square, [XK(c, tb)], ["SQ%d" % si])
                    P.mm(ps[:, :], ONES, SQ[:, si, :], c == 0, c == 7, ["SQ%d" % si, "ONES"], [pk])
                ri = st["rs"] % 2
                st["rs"] += 1
                P.act(RS[:, ri, :], ps[:, :], AF.Ln, [pk, "CST"], ["RS%d" % ri], bias=EPSAP, scale=1.0 / D)
                P.act(RS[:, ri, :], RS[:, ri, :], AF.Exp, ["RS%d" % ri], ["RS%d" % ri], scale=-0.5)
                for c in range(8):
                    dap, dkey = dst(c, tb)
                    P.stt("dve", dap, X[:, c, tbs(tb)], PAR[:, l, g0 + c:g0 + c + 1], RS[:, ri, :],
                          ALU.mult, ALU.mult, [XK(c, tb), "PAR", "RS%d" % ri], [dkey])

        def pre_norm_parts(l, gname, tb, dst):
            g0 = PC[gname]
            stt_ = {}
            slots = [(st["sq"] + i) % 4 for i in range(4)]
            st["sq"] += 4

            def sq(c0):
                def f():
                    for c in range(c0, c0 + 4):
                        si = slots[c - c0]
                        P.act(SQ[:, si, :], X[:, c, tbs(tb)], AF.Square, [XK(c, tb)], ["SQ%d" % si])
                return f

            def mm(c0):
                def f():
                    if c0 == 0:
                        stt_["ps"] = psum()
                    ps, pk = stt_["ps"]
                    for c in range(c0, c0 + 4):
                        si = slots[c - c0]
                        P.mm(ps[:, :], ONES, SQ[:, si, :], c == 0, c == 7, ["SQ%d" % si, "ONES"], [pk])
                    if c0 == 4:
                        ri = st["rs"] % 2
                        st["rs"] += 1
                        P.act(RS[:, ri, :], ps[:, :], AF.Ln, [pk, "CST"], ["RS%d" % ri], bias=EPSAP, scale=1.0 / D)
                        P.act(RS[:, ri, :], RS[:, ri, :], AF.Exp, ["RS%d" % ri], ["RS%d" % ri], scale=-0.5)
                        for c in range(8):
                            dap, dkey = dst(c, tb)
                            P.stt("dve", dap, X[:, c, tbs(tb)], PAR[:, l, g0 + c:g0 + c + 1], RS[:, ri, :],
                                  ALU.mult, ALU.mult, [XK(c, tb), "PAR", "RS%d" % ri], [dkey])
                return f
            return [sq(0), mm(0), sq(4), mm(4)]

        def post_norm_tb(l, gname, tb, ysrc):
            g0 = PC[gname]
            ps, pk = psum()
            for c in range(8):
                yap, yk = ysrc(c)
                si = st["sq"] % 4
                st["sq"] += 1
                P.act(SQ[:, si, :], yap, AF.Square, [yk], ["SQ%d" % si])
                P.mm(ps[:, :], ONES, SQ[:, si, :], c == 0, c == 7, ["SQ%d" % si, "ONES"], [pk])
            ri = st["rs"] % 2
            st["rs"] += 1
            P.act(RS[:, ri, :], ps[:, :], AF.Ln, [pk, "CST"], ["RS%d" % ri], bias=EPSAP, scale=1.0 / D)
            P.act(RS[:, ri, :], RS[:, ri, :], AF.Exp, ["RS%d" % ri], ["RS%d" % ri], scale=-0.5)
            for c in range(8):
                yap, yk = ysrc(c)
                P.stt("dve", yap, yap, PAR[:, l, g0 + c:g0 + c + 1], RS[:, ri, :], ALU.mult, ALU.mult,
                      [yk, "PAR", "RS%d" % ri], [yk])
                P.tt("dve", X[:, c, tbs(tb)], X[:, c, tbs(tb)], yap, ALU.add, [XK(c, tb), yk], [XK(c, tb)])

        PRE = {}

        def preissue(src2d, cols):
            for col0 in cols:
                PRE[repr(src2d[col0 // 128])] = load_w(src2d, col0)

        def load_w(src2d, col0, ncols=128):
            assert ncols == 128
            pk_ = repr(src2d[col0 // 128])
            if pk_ in PRE:
                return PRE.pop(pk_)
            b = st["ws"] % 4
            st["ws"] += 1
            k = "WS%d" % b
            P.dma("pool", WS[:, b, :, :], src2d[col0 // 128], [], [k], "ws%d" % b)
            return b, k

        def stream(tasks, loadfn, computefn, ahead=3):
            pend = [loadfn(t) for t in tasks[:ahead]]
            for i, t in enumerate(tasks):
                if i + ahead < len(tasks):
                    pend.append(loadfn(tasks[i + ahead]))
                computefn(t, pend[i])

        def proj(ps_ap, pk, wb, wk, wcol, ncol_w, rhs_fn, nk=8):
            for kc in range(nk):
                rap, rk = rhs_fn(kc)
                P.mm(ps_ap, WS[:, wb, kc, wcol:wcol + ncol_w], rap, kc == 0, kc == nk - 1, [wk, rk], [pk])

        def proj_out(l, wsrc, rhs_of, gname, YT, next_pre=None):
            groups = [[0, 1], [2], [3]]
            tasks = [(gi, oc) for gi in range(len(groups)) for oc in range(8)]
            hk_pre, hk_post = {}, {}
            if next_pre is not None:
                g2_, dst_, tbl_ = next_pre
                if 0 in tbl_ and 1 in tbl_:
                    pa = pre_norm_parts(l, g2_, 0, dst_)
                    pb_ = pre_norm_parts(l, g2_, 1, dst_)
                    hk_pre[(1, 0)] = [pa[0]]
                    hk_post[(1, 1)] = [pa[1], pa[2]]
                    hk_post[(1, 3)] = [pa[3], pb_[0]]
                    hk_post[(1, 5)] = [pb_[1], pb_[2]]
                    hk_post[(1, 6)] = [pb_[3]]
                if 2 in tbl_:
                    pc_ = pre_norm_parts(l, g2_, 2, dst_)
                    hk_post[(2, 0)] = [pc_[0]]
                    hk_post[(2, 2)] = [pc_[1], pc_[2]]
                    hk_post[(2, 4)] = [pc_[3]]

            def comp(t, wbk):
                gi, oc = t
                for f in hk_pre.get(t, ()):
                    f()
                for tq, tb in enumerate(groups[gi]):
                    ps, pk = psum()
                    proj(ps[:, :], pk, wbk[0], wbk[1], 0, 128, lambda kc: rhs_of(kc, tb))
                    P.copy("act", YT[:, oc, tq, :], ps[:, :], [pk], ["YT%d.%d" % (oc, tq)])
                for f in hk_post.get(t, ()):
                    f()
                if oc == 7:
                    for tq, tb in enumerate(groups[gi]):
                        post_norm_tb(l, gname, tb, lambda c: (YT[:, c, tq, :], "YT%d.%d" % (c, tq)))

            stream(tasks, lambda t: load_w(wsrc, 128 * t[1]), comp)

        for l in range(nl):
            pre_norm(l, "g_mix_pre")
            CAT = z_bf(0, 8 * S).rearrange("p (c t) -> p c t", c=8)
            VA = z_bf(0, 2 * 16 * 256).rearrange("p (b k h c) -> p b k h c", b=2, k=16, h=2)
            FS = z_f32(32768, 4 * 2080).rearrange("p (s t) -> p s t", s=4)
            BS = z_bf(32768 + 33280, 2 * S).rearrange("p (s t) -> p s t", s=2)
            WV = z_bf(32768 + 33280 + 8192, 8 * 128).rearrange("p (k n) -> p k n", k=8)
            PT = z_bf(32768 + 33280 + 8192 + 2048, 4 * TB).rearrange("p (a t) -> p a t", a=4)
            GW = z_bf(32768 + 33280 + 8192 + 2048 + 4096, 4 * 128).rearrange("p (a n) -> p a n", a=4)
            RT = FS[:, 3, 32:32 + 4 * TB].rearrange("p (a t) -> p a t", a=4)

            def CK(c, tb):
                return "C%d.%d" % (c, tb)

            Hrhs = lambda tb: (lambda kc: (H[:, kc, tbs(tb)], HK(kc, tb)))
            ALLH = [HK(c, tb) for c in range(8) for tb in range(NTB)]

            for a_ in range(4):
                P.dma("pool", GW[:, a_, :], gw[l, a_], [], ["GW"], "gw")
            P.memset("pool", VA.rearrange("p b k h c -> p (b k h c)"), 1.0, ["VA0", "VA1"])
            ACC = [FS[:, 0, 32:32 + S], FS[:, 1, 32:32 + S]]
            R2 = FS[:, 2, 32:32 + S]
            QT = BS[:, 0, :]
            KT = BS[:, 1, :]
            QTK = ["QT%d" % tb for tb in range(NTB)]
            KTK = ["KT%d" % tb for tb in range(NTB)]
            csn_i = 0
            pt_i = 0
            for p in range(4):
                wq = load_w(w_in[l], 1024 + 512 * p)
                wkk = load_w(w_in[l], 1024 + 512 * p + 256)
                P.dma("pool", WV[:, :, :], w_in[l][24 + p],
                      [], ["WV"], "wv")
                csb = {}

                def rope_group(tb, qi_):
                    w1_, dst, dk = ((wq, QT, QTK), (wkk, KT, KTK))[qi_]
                    info = {}

                    def a():
                        if qi_ == 0:
                            cb = st["csn"] % 2
                            st["csn"] += 1
                            csb[tb] = cb
                            P.dma("sp", CSN[:, cb, 0, :], cos_d[:, tbs(tb)], [], ["CSN%d" % cb], "csn%d" % cb)
                            P.dma("sp", CSN[:, cb, 1, :], sin_d[:, tbs(tb)], [], ["CSN%d" % cb], "csn%d" % cb)
                        ps, pk = psum()
                        proj(ps[:, :], pk, w1_[0], w1_[1], 0, 128, Hrhs(tb))
                        P.copy("act", PT[:, qi_, :], ps[:, :], [pk], ["PT%d" % qi_])
                        info["ps"] = (ps, pk)

                    def b():
                        ps, pk = info["ps"]
                        cb = csb[tb]
                        ps2, pk2 = psum()
                        P.mm(ps2[:, :], PERMT, PT[:, qi_, :], True, True, ["PERMT", "PT%d" % qi_], [pk2])
                        ra, rb = 2 * qi_, 2 * qi_ + 1
                        P.tt("dve", RT[:, ra, :], ps[:, :], CSN[:, cb, 0, :], ALU.mult, [pk, "CSN%d" % cb], ["RT%d" % ra])
                        P.tt("dve", RT[:, rb, :], ps2[:, :], CSN[:, cb, 1, :], ALU.mult, [pk2, "CSN%d" % cb], ["RT%d" % rb])
                        P.tt("dve", dst[:, tbs(tb)], RT[:, ra, :], RT[:, rb, :], ALU.add, ["RT%d" % ra, "RT%d" % rb], [dk[tb]])
                    return a, b

                rgs = [rope_group(tb, qi_) for tb in range(NTB) for qi_ in range(2)]
                KTPf = FS[:, 3, 32:32 + S].bitcast(BF16).rearrange("p (a t) -> p a t", a=2)
                KTP = [KTPf[:, 0, :], KTPf[:, 1, :]]
                RTK = ["RT0", "RT1", "RT2", "RT3"]

                def emit_rope_and_v(extra):
                    pend_r = None
                    for gi_, (a_fn, b_fn) in enumerate(rgs):
                        a_fn()
                        if pend_r is not None:
                            pend_r()
                        pend_r = b_fn
                        if gi_ in extra:
                            extra[gi_]()
                    pend_r()
                    P.copy("act", KTP[0].rearrange("p (r i) -> p r i", r=4), KT.rearrange("p (i r) -> p r i", r=4), KTK, RTK)
                units = []
                ring = {}

                def new_pt():
                    pi = st.setdefault("pt", 0) % 4
                    st["pt"] += 1
                    return pi, "PT%d" % pi

                VTf = SQ.rearrange("p a t -> p (a t)")

                def vt_unit():
                    def a():
                        for tb in range(NTB):
                            ps, pk = psum()
                            for kc in range(8):
                                P.mm(ps[:, :], WV[:, kc, :], H[:, kc, tbs(tb)], kc == 0, kc == 7, ["WV", HK(kc, tb)], [pk])
                            P.copy("act", SQ[:, tb, :], ps[:, :], [pk], ["SQ%d" % tb])
                    return ("v", a, None)

                def v_unit(pat):
                    vb = pat % 2
                    vk = "VA%d" % vb

                    def a():
                        for g4 in range(4):
                            ps, pk = psum()
                            for j in range(4):
                                blk = 4 * g4 + j
                                if pat == 0:
                                    tsl = slice(128 * blk, 128 * blk + 128)
                                    rk = ["SQ%d" % (blk // 4)]
                                elif pat == 1:
                                    r_, m_ = blk // 4, blk % 4
                                    tsl = slice(512 * m_ + r_, 512 * (m_ + 1), 4)
                                    rk = ["SQ%d" % m_]
                                else:
                                    tsl = slice(blk, S, 16)
                                    rk = ["SQ%d" % t_ for t_ in range(NTB)]
                                P.mm(ps[:, 128 * j:128 * j + 128], VTf[:, tsl], IDENT, True, True, rk + ["IDENT"], [pk])
                            psv = ps[:, :].rearrange("p (j c) -> p j c", j=4)
                            P.copy("act", VA[:, vb, 4 * g4:4 * g4 + 4, 0, 0:64], psv[:, :, 0:64], [pk], [vk])
                            P.copy("act", VA[:, vb, 4 * g4:4 * g4 + 4, 1, 64:128], psv[:, :, 64:128], [pk], [vk])
                    return ("v", a, None)

                def score_unit(pat, hh, g, lhs_rhs, wdt, mrow):
                    pb = slice(64 * hh, 64 * hh + 64)
                    key = (pat, hh, g)

                    def a():
                        ps, pk = psum()
                        for (c0, ncol, ksrc, ksl, qsl) in lhs_rhs:
                            P.mm(ps[:, c0:c0 + ncol], ksrc[pb, ksl], QT[pb, qsl], True, True, KTK + QTK + RTK, [pk])
                        pi, ptk = new_pt()
                        ring[key] = (pi, ptk)
                        P.act(PT[:, pi, 0:wdt], ps[:, 0:wdt], AF.Exp, [pk], [ptk], scale=0.125)
                        P.tt("dve", PT[:, pi, 0:wdt], PT[:, pi, 0:wdt], MSK[:, mrow, 0:wdt], ALU.mult, [ptk, "MSK"], [ptk])
                    return a

                pso_state = {}
                for pat in range(3):
                    vb = pat % 2
                    vk = "VA%d" % vb
                    if pat == 0:
                        pass
                    elif pat == 1:
                        units.append(v_unit(2))
                    for hh in range(2):
                        acck = "ACC%d" % hh
                        if pat == 0:
                            for g in range(8):
                                mm_list = []
                                for jj in range(2):
                                    j = 2 * g + jj
                                    nq = 256 if j < 15 else 128
                                    mm_list.append((256 * jj, nq, KT, slice(128 * j, 128 * j + 128), slice(128 * j, 128 * j + nq)))
                                a_fn = score_unit(pat, hh, g, mm_list, 512 if g < 7 else 384, 0)

                                def b_fn(pat=pat, hh=hh, g=g, vb=vb, vk=vk, acck=acck):
                                    pi, ptk = ring[(pat, hh, g)]
                                    for jj in range(2):
                                        i = 2 * g + jj
                                        if i % 4 == 0:
                                            pso_state[(pat, hh)] = psum()
                                        pso, pok = pso_state[(pat, hh)]
                                        oc = slice(128 * (i % 4), 128 * (i % 4) + 128)
                                        first = True
                                        if i > 0:
                                            if jj == 0:
                                                pprev, pprevk = ring[(pat, hh, g - 1)]
                                                rhs0 = PT[:, pprev, 384:512]
                                                k0 = pprevk
                                            else:
                                                rhs0 = PT[:, pi, 128:256]
                                                k0 = ptk
                                            P.mm(pso[:, oc], VA[:, vb, i - 1, hh, :], rhs0, True, False, [vk, k0], [pok])
                                            first = False
                                        P.mm(pso[:, oc], VA[:, vb, i, hh, :], PT[:, pi, 256 * jj:256 * jj + 128], first, True,
                                             [vk, ptk], [pok])
                                        if i % 4 == 3:
                                            G_ = i // 4
                                            P.copy("act", ACC[hh][:, 512 * G_:512 * G_ + 512], pso[:, :], [pok], [acck])
                                units.append(("s", a_fn, b_fn))
                        elif pat == 1:
                            for r_ in range(4):
                                for g2 in range(2):
                                    mm_list = []
                                    for jj in range(2):
                                        m_ = 2 * g2 + jj
                                        nq = 256 if m_ < 3 else 128
                                        ksl = slice(512 * r_ + 128 * m_, 512 * r_ + 128 * m_ + 128)
                                        qsl = slice(512 * m_ + r_, min(512 * (m_ + 2), S), 4)
                                        mm_list.append((256 * jj, nq, KTP[0], ksl, qsl))
                                    a_fn = score_unit(pat, hh, (r_, g2), mm_list, 512 if g2 < 1 else 384, 0)

                                    def b_fn(pat=pat, hh=hh, r_=r_, g2=g2, vb=vb, vk=vk, acck=acck):
                                        pi, ptk = ring[(pat, hh, (r_, g2))]
                                        if g2 == 0:
                                            pso_state[(pat, hh)] = psum()
                                        pso, pok = pso_state[(pat, hh)]
                                        for jj in range(2):
                                            i = 2 * g2 + jj
                                            oc = slice(128 * i, 128 * i + 128)
                                            first = True
                                            if i > 0:
                                                if jj == 0:
                                                    pprev, pprevk = ring[(pat, hh, (r_, g2 - 1))]
                                                    rhs0 = PT[:, pprev, 384:512]
                                                    k0 = pprevk
                                                else:
                                                    rhs0 = PT[:, pi, 128:256]
                                                    k0 = ptk
                                                P.mm(pso[:, oc], VA[:, vb, 4 * r_ + i - 1, hh, :], rhs0, True, False, [vk, k0], [pok])
                                                first = False
                                            P.mm(pso[:, oc], VA[:, vb, 4 * r_ + i, hh, :], PT[:, pi, 256 * jj:256 * jj + 128],
                                                 first, True, [vk, ptk], [pok])
                                        if g2 == 1:
                                            accv = ACC[hh][:, r_:S:4]
                                            P.tt("dve", accv, pso[:, :], accv, ALU.add, [pok, acck], [acck])
                                    units.append(("s", a_fn, b_fn))
                        else:
                            for rg in range(4):
                                mm_list = []
                                for jj in range(4):
                                    r_ = 4 * rg + jj
                                    sl = slice(r_, S, 16)
                                    mm_list.append((128 * jj, 128, KTP[1], slice(128 * r_, 128 * r_ + 128), sl))
                                a_fn = score_unit(pat, hh, rg, mm_list, 512, 1)

                                def b_fn(pat=pat, hh=hh, rg=rg, vb=vb, vk=vk, acck=acck):
                                    pi, ptk = ring[(pat, hh, rg)]
                                    pso, pok = psum()
                                    for jj in range(4):
                                        r_ = 4 * rg + jj
                                        P.mm(pso[:, 128 * jj:128 * jj + 128], VA[:, vb, r_, hh, :], PT[:, pi, 128 * jj:128 * jj + 128],
                                             True, True, [vk, ptk], [pok])
                                    accv = ACC[hh].rearrange("p (j r) -> p r j", r=16)[:, 4 * rg:4 * rg + 4, :]
                                    psov = pso[:, :].rearrange("p (r j) -> p r j", r=4)
                                    P.tt("dve", accv, psov, accv, ALU.add, [pok, acck], [acck])
                                units.append(("s", a_fn, b_fn))
                emit_rope_and_v({1: vt_unit()[1], 3: v_unit(0)[1], 5: v_unit(1)[1]})
                def ktp3():
                    P.copy("act", KTP[1].rearrange("p (r j) -> p r j", r=16), KT.rearrange("p (j r) -> p r j", r=16), KTK, RTK)
                units.insert(4, ("x", ktp3, None))
                pend_q = []
                for (kind, a_fn, b_fn) in units:
                    if kind == "x":
                        a_fn()
                        continue
                    if kind == "v":
                        while pend_q:
                            pend_q.pop(0)()
                        a_fn()
                    else:
                        a_fn()
                        if len(pend_q) >= 2:
                            pend_q.pop(0)()
                        pend_q.append(b_fn)
                while pend_q:
                    pend_q.pop(0)()

                for (hh_, psl_) in ((0, slice(64, 128)), (1, slice(0, 64))):
                    P.act(ACC[hh_][psl_, :], ACC[hh_][psl_, :], AF.Ln, ["ACC%d" % hh_], ["ACC%d" % hh_])
                    P.act(ACC[hh_][psl_, :], ACC[hh_][psl_, :], AF.Exp, ["ACC%d" % hh_], ["ACC%d" % hh_], scale=-1.0)
                P.dma("sp", R2[0:64, :], ACC[0][64:128, :], ["ACC0"], ["R2"], "r2")
                P.dma("sp", R2[64:128, :], ACC[1][0:64, :], ["ACC1"], ["R2"], "r2")
                P.tt("dve", CAT[0:64, 4 + p, :], ACC[0][0:64, :], R2[0:64, :], ALU.mult, ["ACC0", "R2"],
                     [CK(4 + p, tb) for tb in range(NTB)])
                P.tt("dve", CAT[64:128, 4 + p, :], ACC[1][64:128, :], R2[64:128, :], ALU.mult, ["ACC1", "R2"],
                     [CK(4 + p, tb) for tb in range(NTB)])
            preissue(w_in[l], [0, 256, 128])
            barrier()

            UPB = FS[:, 0, :].bitcast(BF16).rearrange("p (c t) -> p c t", c=2)
            AC = [FS[:, 1, 32:32 + S], FS[:, 2, 32:32 + S]]
            SGT = FS[:, 3, 32:32 + 2 * TB].rearrange("p (a t) -> p a t", a=2)
            DGs = [BS.rearrange("p s t -> p (s t)")[:, 0:31 * 128].rearrange("p (j n) -> p j n", j=31),
                   CAT[:, 2:4, :].rearrange("p c t -> p (c t)")[:, 0:31 * 128].rearrange("p (j n) -> p j n", j=31)]
            P.memset("dve", UPB[:, 0, 0:32], 0.0, ["UP0"])
            P.memset("dve", UPB[:, 1, 0:32], 0.0, ["UP1"])
            cw0 = PC["conv_w"]
            for ch in range(2):
                upk = "UP%d" % ch
                wb, wk = load_w(w_in[l], 128 * ch, 128)
                wb2, wk2 = load_w(w_in[l], 256 + 128 * ch, 128)
                for j in range(31):
                    P.ts("dve", DGs[ch][:, j, :], IDENT, PAR[:, l, cw0 + 31 * ch + j:cw0 + 31 * ch + j + 1], None,
                         ALU.mult, ALU.bypass, ["IDENT", "PAR"], ["DG%d" % ch])
                for tb in range(NTB):
                    ps, pk = psum()
                    proj(ps[:, :], pk, wb, wk, 0, 128, Hrhs(tb))
                    ps2, pk2 = psum()
                    proj(ps2[:, :], pk2, wb2, wk2, 0, 128, Hrhs(tb))
                    si = tb % 2
                    P.act(SGT[:, si, :], ps2[:, :], AF.Sigmoid, [pk2], ["SGT%d" % si])
                    P.tt("dve", UPB[:, ch, 32 + tb * TB:32 + (tb + 1) * TB], ps[:, :], SGT[:, si, :], ALU.mult,
                         [pk, "SGT%d" % si], [upk])

            def conv_tb(ch, tb):
                ps, pk = psum()
                for j in range(31):
                    P.mm(ps[:, :], DGs[ch][:, j, :], UPB[:, ch, 2 + j + tb * TB:2 + j + (tb + 1) * TB], j == 0, j == 30,
                         ["DG%d" % ch, "UP%d" % ch], [pk])
                P.act(AC[ch][:, tbs(tb)], ps[:, :], AF.Identity, [pk, "PAR"], ["AC%d.%d" % (ch, tb)],
                      bias=PAR[:, l, PC["conv_b"] + ch:PC["conv_b"] + ch + 1], scale=1.0)

            def ln_tb(tb):
                aks = ["AC%d.%d" % (ch, tb) for ch in range(2)]
                ps, pk = psum()
                for ch in range(2):
                    si = st["sq"] % 4
                    st["sq"] += 1
                    P.copy("act", SQ[:, si, :], AC[ch][:, tbs(tb)], [aks[ch]], ["SQ%d" % si])
                    P.mm(ps[:, :], ONES, SQ[:, si, :], ch == 0, ch == 1, ["SQ%d" % si, "ONES"], [pk])
                for ch in range(2):
                    P.stt("dve", AC[ch][:, tbs(tb)], ps[:, :], -1.0 / 256.0, AC[ch][:, tbs(tb)], ALU.mult, ALU.add,
                          [pk, aks[ch]], [aks[ch]])
                ps2, pk2 = psum()
                for ch in range(2):
                    si = st["sq"] % 4
                    st["sq"] += 1
                    P.act(SQ[:, si, :], AC[ch][:, tbs(tb)], AF.Square, [aks[ch]], ["SQ%d" % si])
                    P.mm(ps2[:, :], ONES, SQ[:, si, :], ch == 0, ch == 1, ["SQ%d" % si, "ONES"], [pk2])
                ri = st["rs"] % 2
                st["rs"] += 1
                P.act(RS[:, ri, :], ps2[:, :], AF.Ln, [pk2, "CST"], ["RS%d" % ri], bias=EPSAP, scale=1.0 / 256.0)
                P.act(RS[:, ri, :], RS[:, ri, :], AF.Exp, ["RS%d" % ri], ["RS%d" % ri], scale=-0.5)
                for ch in range(2):
                    P.tt("dve", AC[ch][:, tbs(tb)], AC[ch][:, tbs(tb)], RS[:, ri, :], ALU.mult,
                         [aks[ch], "RS%d" % ri], [aks[ch]])
                    P.act(CAT[:, ch, tbs(tb)], AC[ch][:, tbs(tb)], AF.Silu, [aks[ch], "PAR"], [CK(ch, tb)],
                          bias=PAR[:, l, PC["conv_ln_b"] + ch:PC["conv_ln_b"] + ch + 1],
                          scale=PAR[:, l, PC["conv_ln_g"] + ch:PC["conv_ln_g"] + ch + 1])

            for tb in range(NTB):
                conv_tb(0, tb)
                conv_tb(1, tb)
                if tb > 0:
                    ln_tb(tb - 1)
            ln_tb(NTB - 1)
            preissue(w_in[l], [512, 768, 640])
            barrier()

            BXP = FS[:, 0, :]
            P.memset("dve", FS[:, 0, 0:32], 0.0, ["BXP"])
            T1 = FS[:, 0, 32:32 + S]
            U = FS[:, 1, 32:32 + S]
            T2 = FS[:, 2, 32:32 + S]
            T3 = FS[:, 3, 32:32 + S]
            UB = BS[:, 0, :]
            BG = BS[:, 1, :]
            lam = PAR[:, l, PC["lru_lambda"]:PC["lru_lambda"] + 2]
            P.act(CST[:, 4:6], lam, AF.Exp, ["PAR"], ["CL"], scale=-1.0)
            P.act(CST[:, 4:6], CST[:, 4:6], AF.Ln, ["CL", "CST"], ["CL"], bias=ONEAP, scale=1.0)
            P.ts("dve", CST[:, 6:8], CST[:, 4:6], -16.0, None, ALU.mult, ALU.bypass, ["CL"], ["CL2"])
            P.ts("dve", CST[:, 4:6], CST[:, 4:6], -8.0, None, ALU.mult, ALU.bypass, ["CL", "CL2"], ["CL"])
            lw0 = PC["lru_conv_w"]
            for ch in range(2):
                wb, wk = load_w(w_in[l], 512 + 128 * ch, 128)
                wb2, wk2 = load_w(w_in[l], 768 + 128 * ch, 128)
                for tb in range(NTB):
                    ps, pk = psum()
                    proj(ps[:, :], pk, wb, wk, 0, 128, Hrhs(tb))
                    P.copy("act", BXP[:, 32 + tb * TB:32 + (tb + 1) * TB], ps[:, :], [pk], ["BXP"])
                    ps2, pk2 = psum()
                    proj(ps2[:, :], pk2, wb2, wk2, 0, 128, Hrhs(tb))
                    P.act(BG[:, tbs(tb)], ps2[:, :], AF.Gelu_apprx_tanh, [pk2], ["BG"])
                P.ts("dve", U, BXP[:, 29:29 + S], PAR[:, l, lw0 + 4 * ch:lw0 + 4 * ch + 1],
                     PAR[:, l, PC["lru_conv_b"] + ch:PC["lru_conv_b"] + ch + 1], ALU.mult, ALU.add, ["BXP", "PAR"], ["U"])
                for j in range(1, 4):
                    P.stt("dve", U, BXP[:, 29 + j:29 + j + S], PAR[:, l, lw0 + 4 * ch + j:lw0 + 4 * ch + j + 1], U,
                          ALU.mult, ALU.add, ["BXP", "PAR", "U"], ["U"])
                P.copy("act", UB, U, ["U"], ["UB"])
                for tb in range(NTB):
                    ps, pk = psum()
                    P.mm(ps[:, :], GW[:, ch, :], UB[:, tbs(tb)], True, True, ["GW", "UB"], [pk])
                    P.act(T1[:, tbs(tb)], ps[:, :], AF.Sigmoid, [pk, "PAR"], ["BXP"],
                          bias=PAR[:, l, PC["lru_b_a"] + ch:PC["lru_b_a"] + ch + 1], scale=1.0)
                    ps2, pk2 = psum()
                    P.mm(ps2[:, :], GW[:, 2 + ch, :], UB[:, tbs(tb)], True, True, ["GW", "UB"], [pk2])
                    P.act(T3[:, tbs(tb)], ps2[:, :], AF.Sigmoid, [pk2, "PAR"], ["T3"],
                          bias=PAR[:, l, PC["lru_b_i"] + ch:PC["lru_b_i"] + ch + 1], scale=1.0)
                P.act(T2, T1, AF.Exp, ["BXP", "CL"], ["T2"], scale=CST[:, 4 + ch:5 + ch])
                P.act(T1, T1, AF.Exp, ["BXP", "CL2"], ["BXP"], scale=CST[:, 6 + ch:7 + ch])
                P.act(T1, T1, AF.Sqrt, ["BXP", "CST"], ["BXP"], bias=ONEAP, scale=-1.0)
                P.tt("dve", T3, T3, U, ALU.mult, ["T3", "U"], ["T3"])
                P.tt("dve", T3, T3, T1, ALU.mult, ["T3", "BXP"], ["T3"])
                P.scan(T1, T2, T3, ["T2", "T3"], ["BXP"])
                P.tt("dve", CAT[:, 2 + ch, :], T1, BG, ALU.mult, ["BXP", "BG"], [CK(2 + ch, tb) for tb in range(NTB)])
            preissue(w_out[l], [0, 128, 256])
            barrier()

            YT = FS.rearrange("p s t -> p (s t)")[:, 0:16 * TB].rearrange("p (c q t) -> p c q t", c=8, q=2)
            proj_out(l, w_out[l], lambda kc, tb: (CAT[:, kc, tbs(tb)], CK(kc, tb)), "g_mix_post", YT,
                     next_pre=("g_mem_pre", lambda c, tb: (H[:, c, tbs(tb)], HK(c, tb)), [0, 1, 2]))
            pre_norm(l, "g_mem_pre", [3])
            preissue(w_mq[l], [0, 128])
            barrier()
            if ("x1_%d" % l) in tap_out:
                for c in range(8):
                    P.dma("sp", tap_out["x1_%d" % l][c * 128:(c + 1) * 128, :], X[:, c, :], [XK(c, tb) for tb in range(NTB)], [], "out")
                barrier()

            QM = z_bf(0, 8 * S).rearrange("p (c t) -> p c t", c=8)
            MT = z_f32(32768, 8 * NMEM).rearrange("p (c t) -> p c t", c=8)
            MN = z_bf(32768 + 8192, 8 * NMEM).rearrange("p (c t) -> p c t", c=8)
            KM = z_bf(32768 + 8192 + 4096, 8 * NMEM).rearrange("p (c t) -> p c t", c=8)
            VM = z_bf(32768 + 8192 + 8192, 2 * D).rearrange("p (m n) -> p m n", m=2)
            PM = z_bf(32768 + 8192 + 12288, 4 * TB).rearrange("p (a t) -> p a t", a=4)
            DN = z_f32(32768 + 8192 + 16384, 2 * TB).rearrange("p (a t) -> p a t", a=2)
            for c in range(8):
                P.dma("sp", MT[:, c, :], memT[c * 128:(c + 1) * 128, :], [], ["MT"], "mt")

            def comp_q(oc, wbk):
                for tb in range(NTB):
                    ps, pk = psum()
                    proj(ps[:, :], pk, wbk[0], wbk[1], 0, 128, Hrhs(tb))
                    P.copy("act", QM[:, oc, tbs(tb)], ps[:, :], [pk], ["QM%d.%d" % (oc, tb)])

            stream([0, 1], lambda oc: load_w(w_mq[l], 128 * oc), comp_q)
            ps, pk = psum()
            for c in range(8):
                si = st["sq"] % 4
                st["sq"] += 1
                P.act(SQ[:, si, 0:NMEM], MT[:, c, :], AF.Square, ["MT"], ["SQ%d" % si])
                P.mm(ps[:, 0:NMEM], ONES, SQ[:, si, 0:NMEM], c == 0, c == 7, ["SQ%d" % si, "ONES"], [pk])
            ri = st["rs"] % 2
            st["rs"] += 1
            P.act(RS[:, ri, 0:NMEM], ps[:, 0:NMEM], AF.Ln, [pk, "CST"], ["RS%d" % ri], bias=EPSAP, scale=1.0 / D)
            P.act(RS[:, ri, 0:NMEM], RS[:, ri, 0:NMEM], AF.Exp, ["RS%d" % ri], ["RS%d" % ri], scale=-0.5)
            gk0 = PC["g_mem_kv"]
            for c in range(8):
                P.stt("dve", MN[:, c, :], MT[:, c, :], PAR[:, l, gk0 + c:gk0 + c + 1], RS[:, ri, 0:NMEM], ALU.mult, ALU.mult,
                      ["MT", "PAR", "RS%d" % ri], ["MN%d" % c])
            stream([2, 3], lambda oc: load_w(w_mq[l], 128 * oc), comp_q)

            def comp_k(oc, wbk):
                ps, pk = psum()
                proj(ps[:, 0:NMEM], pk, wbk[0], wbk[1], 0, 128, lambda kc: (MN[:, kc, :], "MN%d" % kc))
                P.copy("act", KM[:, oc, :], ps[:, 0:NMEM], [pk], ["KM%d" % oc])

            stream(list(range(8)), lambda oc: load_w(w_mkv[l], 128 * oc), comp_k)

            def comp_v(n, wbk):
                ps, pk = psum()
                for mt in range(2):
                    for kc in range(8):
                        P.mm(ps[:, 128 * mt:128 * mt + 128], MN[:, kc, 128 * mt:128 * mt + 128], WS[:, wbk[0], kc, :], kc == 0, kc == 7,
                             ["MN%d" % kc, wbk[1]], [pk])
                psv = ps[:, 0:256].rearrange("p (m c) -> p m c", m=2)
                P.copy("act", VM[:, :, 128 * n:128 * n + 128], psv, [pk], ["VM"])

            stream(list(range(8)), lambda n: load_w(w_mkv[l], D + 128 * n), comp_v)
            stream([4, 5, 6, 7], lambda oc: load_w(w_mq[l], 128 * oc), comp_q)
            cm = {"pm": 0}

            def cross_unit(hd, tb):
                info = {}

                def a():
                    pks = []
                    for mt in range(2):
                        ps, pk = psum()
                        for dc in range(2):
                            P.mm(ps[:, :], KM[:, 2 * hd + dc, 128 * mt:128 * mt + 128], QM[:, 2 * hd + dc, tbs(tb)], dc == 0, dc == 1,
                                 ["KM%d" % (2 * hd + dc), "QM%d.%d" % (2 * hd + dc, tb)], [pk])
                        pi = cm["pm"] % 4
                        cm["pm"] += 1
                        P.act(PM[:, pi, :], ps[:, :], AF.Exp, [pk], ["PM%d" % pi], scale=1.0 / 16.0)
                        pks.append(pi)
                    info["pks"] = pks

                def b():
                    pks = info["pks"]
                    psd, pdk = psum()
                    for mt in range(2):
                        P.mm(psd[:, :], ONES, PM[:, pks[mt], :], mt == 0, mt == 1, ["ONES", "PM%d" % pks[mt]], [pdk])
                    di = (hd * NTB + tb) % 2
                    P.act(DN[:, di, :], psd[:, :], AF.Ln, [pdk], ["DN%d" % di])
                    P.act(DN[:, di, :], DN[:, di, :], AF.Exp, ["DN%d" % di], ["DN%d" % di], scale=-1.0)
                    for dc in range(2):
                        pso, pok = psum()
                        for mt in range(2):
                            P.mm(pso[:, :], VM[:, mt, 256 * hd + 128 * dc:256 * hd + 128 * dc + 128], PM[:, pks[mt], :], mt == 0, mt == 1,
                                 ["VM", "PM%d" % pks[mt]], [pok])
                        P.tt("dve", QM[:, 2 * hd + dc, tbs(tb)], pso[:, :], DN[:, di, :], ALU.mult, [pok, "DN%d" % di],
                             ["QM%d.%d" % (2 * hd + dc, tb)])
                return a, b

            cunits = [cross_unit(hd, tb) for hd in range(4) for tb in range(NTB)]
            pend_b = None
            for (a_fn, b_fn) in cunits:
                a_fn()
                if pend_b is not None:
                    pend_b()
                pend_b = b_fn
            pend_b()
            preissue(w_mo[l], [0, 128, 256])
            barrier()
            YT2 = z_f32(32768, 16 * TB).rearrange("p (c q t) -> p c q t", c=8, q=2)
            HT = 1024
            Hflat = H.rearrange("p c t -> p (c t)")
            HF = Hflat[:, 0:8 * HT].rearrange("p (c t) -> p c t", c=8)
            proj_out(l, w_mo[l], lambda kc, tb: (QM[:, kc, tbs(tb)], "QM%d.%d" % (kc, tb)), "g_mem_post", YT2,
                     next_pre=("g_ffn_pre", lambda c, tb: (HF[:, c, (tb % 2) * TB:(tb % 2 + 1) * TB], "HF%d.%d" % (c, tb % 2)),
                               [0, 1]))
            preissue(w_up[l], [0, DFF, 128])
            barrier()
            if ("x2_%d" % l) in tap_out:
                for c in range(8):
                    P.dma("sp", tap_out["x2_%d" % l][c * 128:(c + 1) * 128, :], X[:, c, :], [XK(c, tb) for tb in range(NTB)], [], "out")
                barrier()

            YFB = Hflat[:, 8 * HT:16 * HT].bitcast(F32).rearrange("p (c t) -> p c t", c=8)
            G = z_bf(0, NGC * HT).rearrange("p (c t) -> p c t", c=NGC)
            WD = z_bf(45056, 2 * NGC * 128).rearrange("p (b c n) -> p b c n", b=2, c=NGC)
            YP = CSNRAW[:, 0:4 * (TB + 2)].rearrange("p (a b t) -> p a b t", a=2, b=2)
            UG = z_f32(45056 + 11264, 2 * 2 * TB).rearrange("p (a b t) -> p a b t", a=2, b=2)
            YFA = z_f32(45056 + 11264 + 8192, 8 * TB).rearrange("p (c t) -> p c t", c=8)
            HALO = z_f32(45056 + 11264 + 8192 + 16384, 2 * NGC * 2).rearrange("p (c t) -> p c t", c=2 * NGC)
            YFv = [YFA, YFB]
            SQF = SQ.rearrange("p a t -> p (a t)").bitcast(F32).rearrange("p (a t) -> p a t", a=2)
            fw0 = PC["ffn_conv_w"]
            fb0 = PC["ffn_conv_b"]
            for half in range(2):
                tasks = [(gc, which) for gc in range(NGC) for which in range(2)]

                def comp_up(t, wbk):
                    gc, which = t
                    b_, wk = wbk
                    cc = gc + NGC * which
                    ypk = "YP%d" % which
                    for tq in range(2):
                        ps, pk = psum()
                        proj(ps[:, :], pk, b_, wk, 0, 128, lambda kc: (HF[:, kc, tq * TB:(tq + 1) * TB], "HF%d.%d" % (kc, tq)))
                        hkey = "YH%d.%d" % (which, tq)
                        yk = ypk + ".%d" % tq
                        P.copy("act", YP[:, which, tq, 2:2 + TB], ps[:, :], [pk], [yk])
                        if tq == 0 and half == 0:
                            pass
                        elif tq == 0:
                            P.copy("act", YP[:, which, 0, 0:2], HALO[:, cc, :], ["HALO%d" % cc], [hkey])
                        else:
                            P.copy("act", YP[:, which, 1, 0:2], YP[:, which, 0, TB:TB + 2], [ypk + ".0"], [hkey])
                            if half == 0:
                                P.copy("act", HALO[:, cc, :], YP[:, which, 1, TB:TB + 2], [yk], ["HALO%d" % cc])
                        if which == 0 and gc % 2 == 1:
                            ug = SQF[:, tq, :]
                            ugk = ["SQ%d" % (2 * tq), "SQ%d" % (2 * tq + 1)]
                        else:
                            ug = UG[:, which, tq, :]
                            ugk = ["UG%d.%d" % (which, tq)]
                        P.act(ug, ps[:, :], AF.Identity, [pk, "PAR"], ugk,
                              bias=PAR[:, l, fb0 + cc:fb0 + cc + 1], scale=PAR[:, l, fw0 + 3 * cc + 2:fw0 + 3 * cc + 3])
                        P.stt("dve", ug, YP[:, which, tq, 0:TB], PAR[:, l, fw0 + 3 * cc:fw0 + 3 * cc + 1], ug,
                              ALU.mult, ALU.add, [yk, hkey, "PAR"] + ugk, ugk)
                        P.stt("dve", ug, YP[:, which, tq, 1:1 + TB], PAR[:, l, fw0 + 3 * cc + 1:fw0 + 3 * cc + 2], ug,
                              ALU.mult, ALU.add, [yk, hkey, "PAR"] + ugk, ugk)
                        def s2(which=which, ug=ug, ugk=ugk, gc=gc, tq=tq):
                            if which == 0:
                                P.act(ug, ug, AF.Gelu_apprx_tanh, ugk, ugk)
                            else:
                                if gc % 2 == 1:
                                    gate, gk = SQF[:, tq, :], ["SQ%d" % (2 * tq), "SQ%d" % (2 * tq + 1)]
                                else:
                                    gate, gk = UG[:, 0, tq, :], ["UG0.%d" % tq]
                                P.tt("dve", G[:, gc, tq * TB:(tq + 1) * TB], gate, UG[:, 1, tq, :], ALU.mult,
                                     gk + ["UG1.%d" % tq], ["G%d.%d" % (gc, tq)])
                        gi["n"] += 1
                        defer.append((gi["n"] + (2 if which == 0 else 1), s2))
                        while defer and defer[0][0] <= gi["n"]:
                            defer.pop(0)[1]()

                defer = []
                gi = {"n": 0}
                if half == 0:
                    P.memset("dve", YP[:, 0, 0, 0:2], 0.0, ["YH0.0"])
                    P.memset("dve", YP[:, 1, 0, 0:2], 0.0, ["YH1.0"])
                stream(tasks, lambda t: load_w(w_up[l], DFF * t[1] + 128 * t[0]), comp_up)
                while defer:
                    defer.pop(0)[1]()
                dst_ = {"i": 0}

                def load_d(oc):
                    db = dst_["i"] % 2
                    dst_["i"] += 1
                    dk = "WD%d" % db
                    P.dma("pool", WD[:, db, :, :], w_dn[l][oc], [], [dk], "wd%d" % db)
                    return db, dk

                hooks_pre, hooks_post = {}, {}
                if half == 0:
                    hf_dst = lambda c, tb: (HF[:, c, (tb % 2) * TB:(tb % 2 + 1) * TB], "HF%d.%d" % (c, tb % 2))
                    pa = pre_norm_parts(l, "g_ffn_pre", 2, hf_dst)
                    pb_ = pre_norm_parts(l, "g_ffn_pre", 3, hf_dst)
                    hooks_pre[1] = [pa[0]]
                    hooks_post[1] = [pa[1], pa[2]]
                    hooks_post[2] = [pa[3], pb_[0]]
                    hooks_post[3] = [pb_[1], pb_[2]]
                    hooks_post[4] = [pb_[3]]

                def comp_d(oc, wbk):
                    db, dk = wbk
                    for f in hooks_pre.get(oc, ()):
                        f()
                    for tq in range(2):
                        ps, pk = psum()
                        for gc in range(NGC):
                            P.mm(ps[:, :], WD[:, db, gc, :], G[:, gc, tq * TB:(tq + 1) * TB], gc == 0, gc == NGC - 1,
                                 [dk, "G%d.%d" % (gc, tq)], [pk])
                        P.copy("act", YFv[tq][:, oc, :], ps[:, :], [pk], ["YF%d.%d" % (oc, tq)])
                    for f in hooks_post.get(oc, ()):
                        f()

                stream(list(range(8)), load_d, comp_d, ahead=1)
                for tq in range(2):
                    post_norm_tb(l, "g_ffn_post", 2 * half + tq, lambda c: (YFv[tq][:, c, :], "YF%d.%d" % (c, tq)))
            if l + 1 < nl:
                preissue(w_in[l + 1], [1024, 1280])
            barrier()
            if ("x3_%d" % l) in tap_out:
                for c in range(8):
                    P.dma("sp", tap_out["x3_%d" % l][c * 128:(c + 1) * 128, :], X[:, c, :], [XK(c, tb) for tb in range(NTB)], [], "out")
                barrier()

        for c in range(8):
            P.dma("sp", outT[c * 128:(c + 1) * 128, :], X[:, c, :], [XK(c, tb) for tb in range(NTB)], [], "out")

        with nc.Block() as block:
            P.emit(nc, block)
        P._stack.close()
    return nc


def _host_tables():
    inv = (1.0 / (np.float32(10000.0) ** (np.arange(0, 64, 2, dtype=np.float32) / np.float32(64)))).astype(np.float32)
    ang = (np.arange(S, dtype=np.float32)[:, None] * inv[None, :]).astype(np.float32)
    cos = np.cos(ang).astype(np.float32)
    sin = np.sin(ang).astype(np.float32)
    d = np.arange(128) % 64
    cosT = np.ascontiguousarray(cos[:, d % 32].T)
    sgn = np.where(d < 32, -1.0, 1.0).astype(np.float32)
    sinT = np.ascontiguousarray((sin[:, d % 32] * sgn[None, :]).T)
    k = np.arange(128)[:, None]
    q = np.arange(128)[None, :]
    m_same = (q >= k).astype(np.float32)
    m_next = (q <= k).astype(np.float32)
    msk = np.concatenate([m_same, m_next, m_same, m_next, m_same, m_same, m_same, m_same], axis=1)
    return cosT, sinT, np.ascontiguousarray(msk)


def _perm_matrix():
    pm = np.zeros((128, 128), np.float32)
    for m in range(128):
        pm[64 * (m // 64) + ((m % 64) + 32) % 64, m] = 1.0
    return pm


def _prep_shared(inp):
    f = lambda a: np.ascontiguousarray(np.asarray(a, dtype=np.float32))
    w_in = f(inp["w_in"])
    cols = list(range(1024))
    for p in range(4):
        qc = [1024 + 128 * p + i for i in range(128)]
        kc = [1536 + 128 * p + i for i in range(128)]
        sw = [64 * (i // 64) + ((i % 64) + 32) % 64 for i in range(128)]
        cols += qc + [qc[j] for j in sw] + kc + [kc[j] for j in sw]
    cols += list(range(2048, 2560))
    w_in_ext = np.ascontiguousarray(w_in[:, :, cols])
    assert w_in_ext.shape[2] == W_IN_EXT
    par = np.zeros((NL, 128, NPAR), np.float32)

    def vec(name, v, nch):
        par[:, :, PC[name]:PC[name] + nch] = f(v).reshape(NL, nch, 128).transpose(0, 2, 1)

    for n in ("g_mix_pre", "g_mix_post", "g_mem_pre", "g_mem_kv", "g_mem_post", "g_ffn_pre", "g_ffn_post"):
        vec(n, inp[n], 8)
    for n in ("conv_b", "conv_ln_g", "conv_ln_b", "lru_conv_b", "lru_b_a", "lru_b_i", "lru_lambda"):
        vec(n, inp[n], 2)
    vec("ffn_conv_b", inp["ffn_conv_b"], 44)
    cw = f(inp["conv_w"]).reshape(NL, 31, 2, 128).transpose(0, 3, 2, 1).reshape(NL, 128, 62)
    par[:, :, PC["conv_w"]:PC["conv_w"] + 62] = cw
    lw = f(inp["lru_conv_w"]).reshape(NL, 4, 2, 128).transpose(0, 3, 2, 1).reshape(NL, 128, 8)
    par[:, :, PC["lru_conv_w"]:PC["lru_conv_w"] + 8] = lw
    fw = f(inp["ffn_conv_w"]).reshape(NL, 3, 44, 128).transpose(0, 3, 2, 1).reshape(NL, 128, 132)
    par[:, :, PC["ffn_conv_w"]:PC["ffn_conv_w"] + 132] = fw
    gw = np.zeros((NL, 4, 128, 128), np.float32)
    wa = f(inp["lru_w_a"])
    wi = f(inp["lru_w_i"])
    for ch in range(2):
        for hh in range(2):
            gw[:, ch, 64 * hh:64 * hh + 64, 64 * hh:64 * hh + 64] = wa[:, 2 * ch + hh]
            gw[:, 2 + ch, 64 * hh:64 * hh + 64, 64 * hh:64 * hh + 64] = wi[:, 2 * ch + hh]
    cosT, sinT, msk = _host_tables()
    def tile_w(w):
        L_, K_, N_ = w.shape
        return np.ascontiguousarray(w.reshape(L_, K_ // 128, 128, N_ // 128, 128).transpose(0, 3, 2, 1, 4))

    return {
        "par": par, "w_in": tile_w(w_in_ext), "gw": gw, "w_out": tile_w(f(inp["w_out"])), "w_mq": tile_w(f(inp["w_mem_q"])),
        "w_mkv": tile_w(f(inp["w_mem_kv"])), "w_mo": tile_w(f(inp["w_mem_o"])), "w_up": tile_w(f(inp["w_up"])),
        "w_dn": tile_w(f(inp["w_down"])), "cos": cosT, "sin": sinT, "msk": msk,
        "ident": np.eye(128, dtype=np.float32),
        "perm": _perm_matrix(),
    }


def kernel(**inputs):
    shared = _prep_shared(inputs)
    x = np.asarray(inputs["x"], dtype=np.float32)
    mem = np.asarray(inputs["mem"], dtype=np.float32)
    in_maps = []
    for b in range(NCORES):
        m = dict(shared)
        m["xT"] = np.ascontiguousarray(x[b].T)
        m["memT"] = np.ascontiguousarray(mem[b].T)
        in_maps.append(m)
    nc = build()
    res = run_bass_kernel_spmd(nc, in_maps, core_ids=list(range(NCORES)))
    out = np.stack([np.ascontiguousarray(res.results[b]["outT"].T) for b in range(NCORES)], axis=0)
    return out.astype(np.float32)
```

```python
import numpy as np
import concourse.bass as bass
import concourse.mybir as mybir
from concourse.bass_utils import run_bass_kernel_spmd

F32 = mybir.dt.float32
BF16 = mybir.dt.bfloat16
AF = mybir.ActivationFunctionType
ALU = mybir.AluOpType

S = 2048
D = 1024
NL = 2
TB = 512
NTB = 4
NMEM = 256
DFF = 2816
NGC = 22
EPS = 1e-6
NCORES = 8

PC = {}
_o = 0
for _n, _w in [("g_mix_pre", 8), ("g_mix_post", 8), ("g_mem_pre", 8), ("g_mem_kv", 8), ("g_mem_post", 8),
               ("g_ffn_pre", 8), ("g_ffn_post", 8), ("conv_w", 62), ("conv_b", 2), ("conv_ln_g", 2),
               ("conv_ln_b", 2), ("lru_conv_w", 8), ("lru_conv_b", 2), ("lru_b_a", 2), ("lru_b_i", 2),
               ("lru_lambda", 2), ("ffn_conv_w", 132), ("ffn_conv_b", 44)]:
    PC[_n] = _o
    _o += _w
NPAR = _o
W_IN_EXT = 3584


class Op:
    __slots__ = ("stream", "fn", "deps", "dma", "chan", "val")

    def __init__(self, stream, fn, deps, dma, chan=None):
        self.stream, self.fn, self.deps, self.dma = stream, fn, deps, dma
        self.chan = (chan or ("dma_" + stream)) if dma else stream
        self.val = None


class Prog:
    def __init__(self):
        self.ops = []
        self.lw = {}
        self.rd = {}
        self.bar = None
        self.last = {}

    def add(self, stream, fn, r=(), w=(), dma=False, chan=None):
        i = len(self.ops)
        deps = set()
        if self.bar is not None:
            deps.add(self.bar)
        for k in r:
            j = self.lw.get(k)
            if j is not None:
                deps.add(j)
            if k.startswith("PS"):
                for j2 in self.rd.get(k, ()):
                    if self.ops[j2].stream != stream:
                        deps.add(j2)
        for k in w:
            j = self.lw.get(k)
            if j is not None:
                deps.add(j)
            deps.update(self.rd.get(k, ()))
        for k in r:
            self.rd.setdefault(k, []).append(i)
        for k in w:
            self.lw[k] = i
            self.rd[k] = []
        op = Op(stream, fn, deps, dma, chan)
        self.ops.append(op)
        self.last[op.chan] = i
        return i

    def barrier(self, dummy_ap):
        deps = set(self.last.values())
        i = len(self.ops)
        op = Op("pool", lambda e: e.memset(dummy_ap, 0.0), deps, False)
        self.ops.append(op)
        self.last[op.chan] = i
        self.bar = i
        self.lw = {}
        self.rd = {}

    def mm(self, out, lhsT, rhs, start, stop, r, w):
        self.add("pe", lambda e: e.matmul(out, lhsT, rhs, start=start, stop=stop), r, w)

    def act(self, out, in_, func, r, w, bias=None, scale=None):
        kw = {}
        if bias is not None:
            kw["bias"] = bias
        if scale is not None:
            kw["scale"] = scale
        self.add("act", lambda e: e.activation(out=out, in_=in_, func=func, **kw), r, w)

    def tt(self, eng, out, in0, in1, op, r, w):
        self.add(eng, lambda e: e.tensor_tensor(out=out, in0=in0, in1=in1, op=op), r, w)

    def ts(self, eng, out, in0, s1, s2, op0, op1, r, w):
        self.add(eng, lambda e: e.tensor_scalar(out=out, in0=in0, scalar1=s1, scalar2=s2, op0=op0, op1=op1), r, w)

    def stt(self, eng, out, in0, scalar, in1, op0, op1, r, w):
        self.add(eng, lambda e: e.scalar_tensor_tensor(out=out, in0=in0, scalar=scalar, in1=in1, op0=op0, op1=op1), r, w)

    def copy(self, eng, out, in_, r, w):
        if eng == "act":
            self.add("act", lambda e: e.copy(out=out, in_=in_), r, w)
        else:
            self.add(eng, lambda e: e.tensor_copy(out=out, in_=in_), r, w)

    def recip(self, out, in_, r, w):
        self.add("dve", lambda e: e.reciprocal(out=out, in_=in_), r, w)

    def scan(self, out, d0, d1, r, w):
        self.add("dve", lambda e: e.tensor_tensor_scan(out=out, data0=d0, data1=d1, initial=0.0,
                                                       op0=ALU.mult, op1=ALU.add), r, w)

    def memset(self, eng, ap, val, w):
        self.add(eng, lambda e: e.memset(ap, val), (), w)

    def dma(self, q, out, in_, r, w, chan):
        self.add(q, lambda e: e.dma_start(out=out, in_=in_), r, w, dma=True, chan="d_" + chan)

    def emit(self, nc, block):
        ops = self.ops
        needed = [False] * len(ops)

        def exempt(op, dj):
            return op.stream == "pe" and dj.stream == "pe" and not op.dma and not dj.dma

        for op in ops:
            for j in op.deps:
                if not exempt(op, ops[j]):
                    needed[j] = True
        cnt = {}
        for i, op in enumerate(ops):
            if op.dma or needed[i]:
                cnt[op.chan] = cnt.get(op.chan, 0) + (16 if op.dma else 1)
                op.val = cnt[op.chan]
        chans = sorted(cnt.keys())
        sems = {}
        import contextlib
        stack = contextlib.ExitStack()
        for c in chans:
            sems[c] = stack.enter_context(nc.semaphore("s_" + c))
        self._stack = stack

        def run_stream(stream, e):
            waited = {}
            for op in ops:
                if op.stream != stream:
                    continue
                reqs = {}
                for j in op.deps:
                    dj = ops[j]
                    if exempt(op, dj):
                        continue
                    if reqs.get(dj.chan, 0) < dj.val:
                        reqs[dj.chan] = dj.val
                for c, v in reqs.items():
                    if waited.get(c, 0) < v:
                        e.wait_ge(sems[c], v)
                        waited[c] = v
                inst = op.fn(e)
                if op.val is not None:
                    inst.then_inc(sems[op.chan], 16 if op.dma else 1)
            if stream == "sp":
                for c in chans:
                    if waited.get(c, 0) < cnt[c]:
                        e.wait_ge(sems[c], cnt[c])

        @block.sync
        def _(e):
            run_stream("sp", e)

        @block.tensor
        def _(e):
            run_stream("pe", e)

        @block.scalar
        def _(e):
            run_stream("act", e)

        @block.vector
        def _(e):
            run_stream("dve", e)

        @block.gpsimd
        def _(e):
            run_stream("pool", e)


def build(nl=NL, taps=()):
    nc = bass.Bass("TRN2", target_bir_lowering=False)
    dr = {}

    def din(name, shape):
        dr[name] = nc.dram_tensor(name, list(shape), F32, kind="ExternalInput").ap()
        return dr[name]

    xT = din("xT", [D, S])
    memT = din("memT", [D, NMEM])
    par = din("par", [NL, 128, NPAR])
    w_in = din("w_in", [NL, W_IN_EXT // 128, 128, 8, 128])
    gw = din("gw", [NL, 4, 128, 128])
    w_out = din("w_out", [NL, 8, 128, 8, 128])
    w_mq = din("w_mq", [NL, 8, 128, 8, 128])
    w_mkv = din("w_mkv", [NL, 16, 128, 8, 128])
    w_mo = din("w_mo", [NL, 8, 128, 8, 128])
    w_up = din("w_up", [NL, 2 * NGC, 128, 8, 128])
    w_dn = din("w_dn", [NL, 8, 128, NGC, 128])
    cos_d = din("cos", [128, S])
    sin_d = din("sin", [128, S])
    msk_d = din("msk", [128, 2 * TB])
    ident_d = din("ident", [128, 128])
    perm_d = din("perm", [128, 128])
    outT = nc.dram_tensor("outT", [D, S], F32, kind="ExternalOutput").ap()
    tap_out = {}
    for t in taps:
        tap_out[t] = nc.dram_tensor("tap_" + t, [D, S], F32, kind="ExternalOutput").ap()

    P = Prog()
    import contextlib
    es = contextlib.ExitStack()
    with es:
        NA = 52600
        arena = es.enter_context(nc.sbuf_tensor("arena", [128, NA], F32))
        pss = [es.enter_context(nc.psum_tensor("ps%d" % i, [128, TB], F32)) for i in range(8)]

        class Ar:
            off = 0

        def a_f32(off, n):
            assert off % 4 == 0
            return arena[:, off // 4: off // 4 + n]

        def a_bf(off, n):
            assert off % 4 == 0 and n % 2 == 0
            return arena[:, off // 4: off // 4 + n // 2].bitcast(BF16)

        def alloc(nbytes):
            o = Ar.off
            Ar.off += (nbytes + 3) // 4 * 4
            assert Ar.off <= NA * 4, (Ar.off, NA * 4)
            return o

        X = a_f32(alloc(8 * S * 4), 8 * S).rearrange("p (c t) -> p c t", c=8)
        H = a_bf(alloc(8 * S * 2), 8 * S).rearrange("p (c t) -> p c t", c=8)
        PAR = a_f32(alloc(NL * NPAR * 4), NL * NPAR).rearrange("p (l n) -> p l n", l=NL)
        MSK = a_bf(alloc(2 * TB * 2), 2 * TB).rearrange("p (a t) -> p a t", a=2)
        ONES = a_bf(alloc(128 * 2), 128)
        IDENT = a_bf(alloc(128 * 2), 128)
        PERMT = a_bf(alloc(128 * 2), 128)
        CST = a_f32(alloc(16 * 4), 16)
        SQ = a_bf(alloc(4 * TB * 2), 4 * TB).rearrange("p (a t) -> p a t", a=4)
        RS = a_f32(alloc(2 * TB * 4), 2 * TB).rearrange("p (a t) -> p a t", a=2)
        WS = a_bf(alloc(4 * 8 * 128 * 2), 4 * 8 * 128).rearrange("p (b k n) -> p b k n", b=4, k=8)
        CSNRAW = a_f32(alloc((2 * 2 * TB + 8) * 4), 2 * 2 * TB + 8)
        CSN = CSNRAW[:, 0:2 * 2 * TB].rearrange("p (b a t) -> p b a t", b=2, a=2)
        ZOFF = Ar.off
        ZSIZE = NA * 4 - ZOFF

        def z_f32(off, n):
            assert off + n * 4 <= ZSIZE, (off, n, ZSIZE)
            return a_f32(ZOFF + off, n)

        def z_bf(off, n):
            assert off + n * 2 <= ZSIZE, (off, n, ZSIZE)
            return a_bf(ZOFF + off, n)

        EPSAP = CST[:, 0:1]
        ONEAP = CST[:, 1:2]
        DUMMY = CST[:, 2:3]

        st = {"ps": 0, "sq": 0, "rs": 0, "ws": 0, "pt": 0, "csn": 0}

        def psum():
            i = st["ps"] % 8
            st["ps"] += 1
            return pss[i], "PS%d" % i

        def tbs(tb):
            return slice(tb * TB, (tb + 1) * TB)

        def barrier():
            P.barrier(DUMMY)

        P.memset("dve", CST[:, 0:1], EPS, ["CST"])
        P.memset("dve", CST[:, 1:2], 1.0, ["CST"])
        P.memset("dve", CST[:, 2:3], 0.0, ["CST"])
        P.memset("dve", ONES, 1.0, ["ONES"])
        for l in range(NL):
            P.dma("sp", PAR[:, l, :], par[l], [], ["PAR"], "setup")
        P.dma("pool", MSK.rearrange("p a t -> p (a t)"), msk_d, [], ["MSK"], "setup2")
        P.dma("pool", IDENT, ident_d, [], ["IDENT"], "setup2")
        P.dma("pool", PERMT, perm_d, [], ["PERMT"], "setup2")
        barrier()
        for c in range(8):
            P.dma("sp", X[:, c, :], xT[c * 128:(c + 1) * 128, :], [], ["X%d.%d" % (c, tb) for tb in range(NTB)], "x%d" % c)

        def XK(c, tb):
            return "X%d.%d" % (c, tb)

        def HK(c, tb):
            return "H%d.%d" % (c, tb)

        def pre_norm(l, gname, tb_list=None, dst=None):
            g0 = PC[gname]
            if tb_list is None:
                tb_list = range(NTB)
            if dst is None:
                dst = lambda c, tb: (H[:, c, tbs(tb)], HK(c, tb))
            for tb in tb_list:
                ps, pk = psum()
                for c in range(8):
                    si = st["sq"] % 4
                    st["sq"] += 1
                    P.act(SQ[:, si, :], X[:, c, tbs(tb)], AF.Square, [XK(c, tb)], ["SQ%d" % si])
                    P.mm(ps[:, :], ONES, SQ[:, si, :], c == 0, c == 7, ["SQ%d" % si, "ONES"], [pk])
                ri = st["rs"] % 2
                st["rs"] += 1
                P.act(RS[:, ri, :], ps[:, :], AF.Ln, [pk, "CST"], ["RS%d" % ri], bias=EPSAP, scale=1.0 / D)
                P.act(RS[:, ri, :], RS[:, ri, :], AF.Exp, ["RS%d" % ri], ["RS%d" % ri], scale=-0.5)
                for c in range(8):
                    dap, dkey = dst(c, tb)
                    P.stt("dve", dap, X[:, c, tbs(tb)], PAR[:, l, g0 + c:g0 + c + 1], RS[:, ri, :],
                          ALU.mult, ALU.mult, [XK(c, tb), "PAR", "RS%d" % ri], [dkey])

        def pre_norm_parts(l, gname, tb, dst):
            g0 = PC[gname]
            stt_ = {}
            slots = [(st["sq"] + i) % 4 for i in range(4)]
            st["sq"] += 4

            def sq(c0):
                def f():
                    for c in range(c0, c0 + 4):
                        si = slots[c - c0]
                        P.act(SQ[:, si, :], X[:, c, tbs(tb)], AF.Square, [XK(c, tb)], ["SQ%d" % si])
                return f

            def mm(c0):
                def f():
                    if c0 == 0:
                        stt_["ps"] = psum()
                    ps, pk = stt_["ps"]
                    for c in range(c0, c0 + 4):
                        si = slots[c - c0]
                        P.mm(ps[:, :], ONES, SQ[:, si, :], c == 0, c == 7, ["SQ%d" % si, "ONES"], [pk])
                    if c0 == 4:
                        ri = st["rs"] % 2
                        st["rs"] += 1
                        P.act(RS[:, ri, :], ps[:, :], AF.Ln, [pk, "CST"], ["RS%d" % ri], bias=EPSAP, scale=1.0 / D)
                        P.act(RS[:, ri, :], RS[:, ri, :], AF.Exp, ["RS%d" % ri], ["RS%d" % ri], scale=-0.5)
                        for c in range(8):
                            dap, dkey = dst(c, tb)
                            P.stt("dve", dap, X[:, c, tbs(tb)], PAR[:, l, g0 + c:g0 + c + 1], RS[:, ri, :],
                                  ALU.mult, ALU.mult, [XK(c, tb), "PAR", "RS%d" % ri], [dkey])
                return f
            return [sq(0), mm(0), sq(4), mm(4)]

        def post_norm_tb(l, gname, tb, ysrc):
            g0 = PC[gname]
            ps, pk = psum()
            for c in range(8):
                yap, yk = ysrc(c)
                si = st["sq"] % 4
                st["sq"] += 1
                P.act(SQ[:, si, :], yap, AF.Square, [yk], ["SQ%d" % si])
                P.mm(ps[:, :], ONES, SQ[:, si, :], c == 0, c == 7, ["SQ%d" % si, "ONES"], [pk])
            ri = st["rs"] % 2
            st["rs"] += 1
            P.act(RS[:, ri, :], ps[:, :], AF.Ln, [pk, "CST"], ["RS%d" % ri], bias=EPSAP, scale=1.0 / D)
            P.act(RS[:, ri, :], RS[:, ri, :], AF.Exp, ["RS%d" % ri], ["RS%d" % ri], scale=-0.5)
            for c in range(8):
                yap, yk = ysrc(c)
                P.stt("dve", yap, yap, PAR[:, l, g0 + c:g0 + c + 1], RS[:, ri, :], ALU.mult, ALU.mult,
                      [yk, "PAR", "RS%d" % ri], [yk])
                P.tt("dve", X[:, c, tbs(tb)], X[:, c, tbs(tb)], yap, ALU.add, [XK(c, tb), yk], [XK(c, tb)])

        PRE = {}

        def preissue(src2d, cols):
            for col0 in cols:
                PRE[repr(src2d[col0 // 128])] = load_w(src2d, col0)

        def load_w(src2d, col0, ncols=128):
            assert ncols == 128
            pk_ = repr(src2d[col0 // 128])
            if pk_ in PRE:
                return PRE.pop(pk_)
            b = st["ws"] % 4
            st["ws"] += 1
            k = "WS%d" % b
            P.dma("pool", WS[:, b, :, :], src2d[col0 // 128], [], [k], "ws%d" % b)
            return b, k

        def stream(tasks, loadfn, computefn, ahead=3):
            pend = [loadfn(t) for t in tasks[:ahead]]
            for i, t in enumerate(tasks):
                if i + ahead < len(tasks):
                    pend.append(loadfn(tasks[i + ahead]))
                computefn(t, pend[i])

        def proj(ps_ap, pk, wb, wk, wcol, ncol_w, rhs_fn, nk=8):
            for kc in range(nk):
                rap, rk = rhs_fn(kc)
                P.mm(ps_ap, WS[:, wb, kc, wcol:wcol + ncol_w], rap, kc == 0, kc == nk - 1, [wk, rk], [pk])

        def proj_out(l, wsrc, rhs_of, gname, YT, next_pre=None):
            groups = [[0, 1], [2], [3]]
            tasks = [(gi, oc) for gi in range(len(groups)) for oc in range(8)]
            hk_pre, hk_post = {}, {}
            if next_pre is not None:
                g2_, dst_, tbl_ = next_pre
                if 0 in tbl_ and 1 in tbl_:
                    pa = pre_norm_parts(l, g2_, 0, dst_)
                    pb_ = pre_norm_parts(l, g2_, 1, dst_)
                    hk_pre[(1, 0)] = [pa[0]]
                    hk_post[(1, 1)] = [pa[1], pa[2]]
                    hk_post[(1, 3)] = [pa[3], pb_[0]]
                    hk_post[(1, 5)] = [pb_[1], pb_[2]]
                    hk_post[(1, 6)] = [pb_[3]]
                if 2 in tbl_:
                    pc_ = pre_norm_parts(l, g2_, 2, dst_)
                    hk_post[(2, 0)] = [pc_[0]]
                    hk_post[(2, 2)] = [pc_[1], pc_[2]]
                    hk_post[(2, 4)] = [pc_[3]]

            def comp(t, wbk):
                gi, oc = t
                for f in hk_pre.get(t, ()):
                    f()
                for tq, tb in enumerate(groups[gi]):
                    ps, pk = psum()
                    proj(ps[:, :], pk, wbk[0], wbk[1], 0, 128, lambda kc: rhs_of(kc, tb))
                    P.copy("act", YT[:, oc, tq, :], ps[:, :], [pk], ["YT%d.%d" % (oc, tq)])
                for f in hk_post.get(t, ()):
                    f()
                if oc == 7:
                    for tq, tb in enumerate(groups[gi]):
                        post_norm_tb(l, gname, tb, lambda c: (YT[:, c, tq, :], "YT%d.%d" % (c, tq)))

            stream(tasks, lambda t: load_w(wsrc, 128 * t[1]), comp)

        for l in range(nl):
            pre_norm(l, "g_mix_pre")
            CAT = z_bf(0, 8 * S).rearrange("p (c t) -> p c t", c=8)
            VA = z_bf(0, 2 * 16 * 256).rearrange("p (b k h c) -> p b k h c", b=2, k=16, h=2)
            FS = z_f32(32768, 4 * 2080).rearrange("p (s t) -> p s t", s=4)
            BS = z_bf(32768 + 33280, 2 * S).rearrange("p (s t) -> p s t", s=2)
            WV = z_bf(32768 + 33280 + 8192, 8 * 128).rearrange("p (k n) -> p k n", k=8)
            PT = z_bf(32768 + 33280 + 8192 + 2048, 4 * TB).rearrange("p (a t) -> p a t", a=4)
            GW = z_bf(32768 + 33280 + 8192 + 2048 + 4096, 4 * 128).rearrange("p (a n) -> p a n", a=4)
            RT = FS[:, 3, 32:32 + 4 * TB].rearrange("p (a t) -> p a t", a=4)

            def CK(c, tb):
                return "C%d.%d" % (c, tb)

            Hrhs = lambda tb: (lambda kc: (H[:, kc, tbs(tb)], HK(kc, tb)))
            ALLH = [HK(c, tb) for c in range(8) for tb in range(NTB)]

            for a_ in range(4):
                P.dma("pool", GW[:, a_, :], gw[l, a_], [], ["GW"], "gw")
            P.memset("pool", VA.rearrange("p b k h c -> p (b k h c)"), 1.0, ["VA0", "VA1"])
            ACC = [FS[:, 0, 32:32 + S], FS[:, 1, 32:32 + S]]
            R2 = FS[:, 2, 32:32 + S]
            QT = BS[:, 0, :]
            KT = BS[:, 1, :]
            QTK = ["QT%d" % tb for tb in range(NTB)]
            KTK = ["KT%d" % tb for tb in range(NTB)]
            csn_i = 0
            pt_i = 0
            norm_pend = []
            for p in range(4):
                wq = load_w(w_in[l], 1024 + 512 * p)
                wkk = load_w(w_in[l], 1024 + 512 * p + 256)
                P.dma("pool", WV[:, :, :], w_in[l][24 + p],
                      [], ["WV"], "wv")
                csb = {}

                def rope_group(tb, qi_):
                    w1_, dst, dk = ((wq, QT, QTK), (wkk, KT, KTK))[qi_]
                    info = {}

                    def a():
                        if qi_ == 0:
                            cb = st["csn"] % 2
                            st["csn"] += 1
                            csb[tb] = cb
                            P.dma("sp", CSN[:, cb, 0, :], cos_d[:, tbs(tb)], [], ["CSN%d" % cb], "csn%d" % cb)
                            P.dma("sp", CSN[:, cb, 1, :], sin_d[:, tbs(tb)], [], ["CSN%d" % cb], "csn%d" % cb)
                        ps, pk = psum()
                        proj(ps[:, :], pk, w1_[0], w1_[1], 0, 128, Hrhs(tb))
                        P.copy("act", PT[:, qi_, :], ps[:, :], [pk], ["PT%d" % qi_])
                        info["ps"] = (ps, pk)

                    def b():
                        ps, pk = info["ps"]
                        cb = csb[tb]
                        ps2, pk2 = psum()
                        P.mm(ps2[:, :], PERMT, PT[:, qi_, :], True, True, ["PERMT", "PT%d" % qi_], [pk2])
                        ra, rb = 2 * qi_, 2 * qi_ + 1
                        P.tt("dve", RT[:, ra, :], ps[:, :], CSN[:, cb, 0, :], ALU.mult, [pk, "CSN%d" % cb], ["RT%d" % ra])
                        P.tt("dve", RT[:, rb, :], ps2[:, :], CSN[:, cb, 1, :], ALU.mult, [pk2, "CSN%d" % cb], ["RT%d" % rb])
                        P.tt("dve", dst[:, tbs(tb)], RT[:, ra, :], RT[:, rb, :], ALU.add, ["RT%d" % ra, "RT%d" % rb], [dk[tb]])
                    return a, b

                rgs = [rope_group(tb, qi_) for tb in range(NTB) for qi_ in range(2)]
                KTPf = FS[:, 3, 32:32 + S].bitcast(BF16).rearrange("p (a t) -> p a t", a=2)
                KTP = [KTPf[:, 0, :], KTPf[:, 1, :]]
                RTK = ["RT0", "RT1", "RT2", "RT3"]

                def emit_rope_and_v(extra):
                    pend_r = None
                    for gi_, (a_fn, b_fn) in enumerate(rgs):
                        a_fn()
                        if pend_r is not None:
                            pend_r()
                        pend_r = b_fn
                        if gi_ in extra:
                            extra[gi_]()
                    pend_r()
                    P.copy("act", KTP[0].rearrange("p (r i) -> p r i", r=4), KT.rearrange("p (i r) -> p r i", r=4), KTK, RTK)
                units = []
                ring = {}

                def new_pt():
                    pi = st.setdefault("pt", 0) % 4
                    st["pt"] += 1
                    return pi, "PT%d" % pi

                VTf = SQ.rearrange("p a t -> p (a t)")

                def vt_unit():
                    def a():
                        for tb in range(NTB):
                            ps, pk = psum()
                            for kc in range(8):
                                P.mm(ps[:, :], WV[:, kc, :], H[:, kc, tbs(tb)], kc == 0, kc == 7, ["WV", HK(kc, tb)], [pk])
                            P.copy("act", SQ[:, tb, :], ps[:, :], [pk], ["SQ%d" % tb])
                    return ("v", a, None)

                def v_unit(pat):
                    vb = pat % 2
                    vk = "VA%d" % vb

                    def a():
                        for g4 in range(4):
                            ps, pk = psum()
                            for j in range(4):
                                blk = 4 * g4 + j
                                if pat == 0:
                                    tsl = slice(128 * blk, 128 * blk + 128)
                                    rk = ["SQ%d" % (blk // 4)]
                                elif pat == 1:
                                    r_, m_ = blk // 4, blk % 4
                                    tsl = slice(512 * m_ + r_, 512 * (m_ + 1), 4)
                                    rk = ["SQ%d" % m_]
                                else:
                                    tsl = slice(blk, S, 16)
                                    rk = ["SQ%d" % t_ for t_ in range(NTB)]
                                P.mm(ps[:, 128 * j:128 * j + 128], VTf[:, tsl], IDENT, True, True, rk + ["IDENT"], [pk])
                            psv = ps[:, :].rearrange("p (j c) -> p j c", j=4)
                            P.copy("act", VA[:, vb, 4 * g4:4 * g4 + 4, 0, 0:64], psv[:, :, 0:64], [pk], [vk])
                            P.copy("act", VA[:, vb, 4 * g4:4 * g4 + 4, 1, 64:128], psv[:, :, 64:128], [pk], [vk])
                    return ("v", a, None)

                def score_unit(pat, hh, g, lhs_rhs, wdt, mrow):
                    pb = slice(64 * hh, 64 * hh + 64)
                    key = (pat, hh, g)

                    def a():
                        ps, pk = psum()
                        for (c0, ncol, ksrc, ksl, qsl) in lhs_rhs:
                            P.mm(ps[:, c0:c0 + ncol], ksrc[pb, ksl], QT[pb, qsl], True, True, KTK + QTK + RTK, [pk])
                        pi, ptk = new_pt()
                        ring[key] = (pi, ptk)
                        P.act(PT[:, pi, 0:wdt], ps[:, 0:wdt], AF.Exp, [pk], [ptk], scale=0.125)
                        P.tt("dve", PT[:, pi, 0:wdt], PT[:, pi, 0:wdt], MSK[:, mrow, 0:wdt], ALU.mult, [ptk, "MSK"], [ptk])
                    return a

                pso_state = {}
                for pat in range(3):
                    vb = pat % 2
                    vk = "VA%d" % vb
                    if pat == 0:
                        pass
                    elif pat == 1:
                        units.append(v_unit(2))
                    for hh in range(2):
                        acck = "ACC%d" % hh
                        if pat == 0:
                            for g in range(8):
                                mm_list = []
                                for jj in range(2):
                                    j = 2 * g + jj
                                    nq = 256 if j < 15 else 128
                                    mm_list.append((256 * jj, nq, KT, slice(128 * j, 128 * j + 128), slice(128 * j, 128 * j + nq)))
                                a_fn = score_unit(pat, hh, g, mm_list, 512 if g < 7 else 384, 0)

                                def b_fn(pat=pat, hh=hh, g=g, vb=vb, vk=vk, acck=acck):
                                    pi, ptk = ring[(pat, hh, g)]
                                    for jj in range(2):
                                        i = 2 * g + jj
                                        if i % 4 == 0:
                                            pso_state[(pat, hh)] = psum()
                                        pso, pok = pso_state[(pat, hh)]
                                        oc = slice(128 * (i % 4), 128 * (i % 4) + 128)
                                        first = True
                                        if i > 0:
                                            if jj == 0:
                                                pprev, pprevk = ring[(pat, hh, g - 1)]
                                                rhs0 = PT[:, pprev, 384:512]
                                                k0 = pprevk
                                            else:
                                                rhs0 = PT[:, pi, 128:256]
                                                k0 = ptk
                                            P.mm(pso[:, oc], VA[:, vb, i - 1, hh, :], rhs0, True, False, [vk, k0], [pok])
                                            first = False
                                        P.mm(pso[:, oc], VA[:, vb, i, hh, :], PT[:, pi, 256 * jj:256 * jj + 128], first, True,
                                             [vk, ptk], [pok])
                                        if i % 4 == 3:
                                            G_ = i // 4
                                            P.copy("act", ACC[hh][:, 512 * G_:512 * G_ + 512], pso[:, :], [pok], [acck])
                                units.append(("s", a_fn, b_fn))
                        elif pat == 1:
                            for r_ in range(4):
                                for g2 in range(2):
                                    mm_list = []
                                    for jj in range(2):
                                        m_ = 2 * g2 + jj
                                        nq = 256 if m_ < 3 else 128
                                        ksl = slice(512 * r_ + 128 * m_, 512 * r_ + 128 * m_ + 128)
                                        qsl = slice(512 * m_ + r_, min(512 * (m_ + 2), S), 4)
                                        mm_list.append((256 * jj, nq, KTP[0], ksl, qsl))
                                    a_fn = score_unit(pat, hh, (r_, g2), mm_list, 512 if g2 < 1 else 384, 0)

                                    def b_fn(pat=pat, hh=hh, r_=r_, g2=g2, vb=vb, vk=vk, acck=acck):
                                        pi, ptk = ring[(pat, hh, (r_, g2))]
                                        if g2 == 0:
                                            pso_state[(pat, hh)] = psum()
                                        pso, pok = pso_state[(pat, hh)]
                                        for jj in range(2):
                                            i = 2 * g2 + jj
                                            oc = slice(128 * i, 128 * i + 128)
                                            first = True
                                            if i > 0:
                                                if jj == 0:
                                                    pprev, pprevk = ring[(pat, hh, (r_, g2 - 1))]
                                                    rhs0 = PT[:, pprev, 384:512]
                                                    k0 = pprevk
                                                else:
                                                    rhs0 = PT[:, pi, 128:256]
                                                    k0 = ptk
                                                P.mm(pso[:, oc], VA[:, vb, 4 * r_ + i - 1, hh, :], rhs0, True, False, [vk, k0], [pok])
                                                first = False
                                            P.mm(pso[:, oc], VA[:, vb, 4 * r_ + i, hh, :], PT[:, pi, 256 * jj:256 * jj + 128],
                                                 first, True, [vk, ptk], [pok])
                                        if g2 == 1:
                                            accv = ACC[hh][:, r_:S:4]
                                            P.tt("dve", accv, pso[:, :], accv, ALU.add, [pok, acck], [acck])
                                    units.append(("s", a_fn, b_fn))
                        else:
                            for rg in range(4):
                                mm_list = []
                                for jj in range(4):
                                    r_ = 4 * rg + jj
                                    sl = slice(r_, S, 16)
                                    mm_list.append((128 * jj, 128, KTP[1], slice(128 * r_, 128 * r_ + 128), sl))
                                a_fn = score_unit(pat, hh, rg, mm_list, 512, 1)

                                def b_fn(pat=pat, hh=hh, rg=rg, vb=vb, vk=vk, acck=acck):
                                    pi, ptk = ring[(pat, hh, rg)]
                                    pso, pok = psum()
                                    for jj in range(4):
                                        r_ = 4 * rg + jj
                                        P.mm(pso[:, 128 * jj:128 * jj + 128], VA[:, vb, r_, hh, :], PT[:, pi, 128 * jj:128 * jj + 128],
                                             True, True, [vk, ptk], [pok])
                                    accv = ACC[hh].rearrange("p (j r) -> p r j", r=16)[:, 4 * rg:4 * rg + 4, :]
                                    psov = pso[:, :].rearrange("p (r j) -> p r j", r=4)
                                    P.tt("dve", accv, psov, accv, ALU.add, [pok, acck], [acck])
                                units.append(("s", a_fn, b_fn))
                emit_rope_and_v({1: vt_unit()[1], 3: v_unit(0)[1], 5: v_unit(1)[1]})
                while norm_pend:
                    norm_pend.pop(0)()
                def ktp3():
                    P.copy("act", KTP[1].rearrange("p (r j) -> p r j", r=16), KT.rearrange("p (j r) -> p r j", r=16), KTK, RTK)
                units.insert(4, ("x", ktp3, None))
                pend_q = []
                for (kind, a_fn, b_fn) in units:
                    if kind == "x":
                        a_fn()
                        continue
                    if kind == "v":
                        while pend_q:
                            pend_q.pop(0)()
                        a_fn()
                    else:
                        a_fn()
                        if len(pend_q) >= 2:
                            pend_q.pop(0)()
                        pend_q.append(b_fn)
                while pend_q:
                    pend_q.pop(0)()

                for (hh_, psl_) in ((0, slice(64, 128)), (1, slice(0, 64))):
                    P.act(ACC[hh_][psl_, :], ACC[hh_][psl_, :], AF.Ln, ["ACC%d" % hh_], ["ACC%d" % hh_])
                    P.act(ACC[hh_][psl_, :], ACC[hh_][psl_, :], AF.Exp, ["ACC%d" % hh_], ["ACC%d" % hh_], scale=-1.0)
                P.dma("sp", R2[0:64, :], ACC[0][64:128, :], ["ACC0"], ["R2"], "r2")
                P.dma("sp", R2[64:128, :], ACC[1][0:64, :], ["ACC1"], ["R2"], "r2")
                def fin_norm(p=p):
                    P.tt("dve", CAT[0:64, 4 + p, :], ACC[0][0:64, :], R2[0:64, :], ALU.mult, ["ACC0", "R2"],
                         [CK(4 + p, tb) for tb in range(NTB)])
                    P.tt("dve", CAT[64:128, 4 + p, :], ACC[1][64:128, :], R2[64:128, :], ALU.mult, ["ACC1", "R2"],
                         [CK(4 + p, tb) for tb in range(NTB)])
                norm_pend.append(fin_norm)
            while norm_pend:
                norm_pend.pop(0)()
            preissue(w_in[l], [0, 256, 128])
            barrier()

            UPB = FS[:, 0, :].bitcast(BF16).rearrange("p (c t) -> p c t", c=2)
            AC = [FS[:, 1, 32:32 + S], FS[:, 2, 32:32 + S]]
            SGT = FS[:, 3, 32:32 + 2 * TB].rearrange("p (a t) -> p a t", a=2)
            DGs = [BS.rearrange("p s t -> p (s t)")[:, 0:31 * 128].rearrange("p (j n) -> p j n", j=31),
                   CAT[:, 2:4, :].rearrange("p c t -> p (c t)")[:, 0:31 * 128].rearrange("p (j n) -> p j n", j=31)]
            P.memset("dve", UPB[:, 0, 0:32], 0.0, ["UP0"])
            P.memset("dve", UPB[:, 1, 0:32], 0.0, ["UP1"])
            cw0 = PC["conv_w"]
            for ch in range(2):
                upk = "UP%d" % ch
                wb, wk = load_w(w_in[l], 128 * ch, 128)
                wb2, wk2 = load_w(w_in[l], 256 + 128 * ch, 128)
                for j in range(31):
                    P.ts("dve", DGs[ch][:, j, :], IDENT, PAR[:, l, cw0 + 31 * ch + j:cw0 + 31 * ch + j + 1], None,
                         ALU.mult, ALU.bypass, ["IDENT", "PAR"], ["DG%d" % ch])
                for tb in range(NTB):
                    ps, pk = psum()
                    proj(ps[:, :], pk, wb, wk, 0, 128, Hrhs(tb))
                    ps2, pk2 = psum()
                    proj(ps2[:, :], pk2, wb2, wk2, 0, 128, Hrhs(tb))
                    si = tb % 2
                    P.act(SGT[:, si, :], ps2[:, :], AF.Sigmoid, [pk2], ["SGT%d" % si])
                    P.tt("dve", UPB[:, ch, 32 + tb * TB:32 + (tb + 1) * TB], ps[:, :], SGT[:, si, :], ALU.mult,
                         [pk, "SGT%d" % si], [upk])

            def conv_tb(ch, tb):
                ps, pk = psum()
                for j in range(31):
                    P.mm(ps[:, :], DGs[ch][:, j, :], UPB[:, ch, 2 + j + tb * TB:2 + j + (tb + 1) * TB], j == 0, j == 30,
                         ["DG%d" % ch, "UP%d" % ch], [pk])
                P.act(AC[ch][:, tbs(tb)], ps[:, :], AF.Identity, [pk, "PAR"], ["AC%d.%d" % (ch, tb)],
                      bias=PAR[:, l, PC["conv_b"] + ch:PC["conv_b"] + ch + 1], scale=1.0)

            def ln_tb(tb):
                aks = ["AC%d.%d" % (ch, tb) for ch in range(2)]
                ps, pk = psum()
                for ch in range(2):
                    si = st["sq"] % 4
                    st["sq"] += 1
                    P.copy("act", SQ[:, si, :], AC[ch][:, tbs(tb)], [aks[ch]], ["SQ%d" % si])
                    P.mm(ps[:, :], ONES, SQ[:, si, :], ch == 0, ch == 1, ["SQ%d" % si, "ONES"], [pk])
                for ch in range(2):
                    P.stt("dve", AC[ch][:, tbs(tb)], ps[:, :], -1.0 / 256.0, AC[ch][:, tbs(tb)], ALU.mult, ALU.add,
                          [pk, aks[ch]], [aks[ch]])
                ps2, pk2 = psum()
                for ch in range(2):
                    si = st["sq"] % 4
                    st["sq"] += 1
                    P.act(SQ[:, si, :], AC[ch][:, tbs(tb)], AF.Square, [aks[ch]], ["SQ%d" % si])
                    P.mm(ps2[:, :], ONES, SQ[:, si, :], ch == 0, ch == 1, ["SQ%d" % si, "ONES"], [pk2])
                ri = st["rs"] % 2
                st["rs"] += 1
                P.act(RS[:, ri, :], ps2[:, :], AF.Ln, [pk2, "CST"], ["RS%d" % ri], bias=EPSAP, scale=1.0 / 256.0)
                P.act(RS[:, ri, :], RS[:, ri, :], AF.Exp, ["RS%d" % ri], ["RS%d" % ri], scale=-0.5)
                for ch in range(2):
                    P.tt("dve", AC[ch][:, tbs(tb)], AC[ch][:, tbs(tb)], RS[:, ri, :], ALU.mult,
                         [aks[ch], "RS%d" % ri], [aks[ch]])
                    P.act(CAT[:, ch, tbs(tb)], AC[ch][:, tbs(tb)], AF.Silu, [aks[ch], "PAR"], [CK(ch, tb)],
                          bias=PAR[:, l, PC["conv_ln_b"] + ch:PC["conv_ln_b"] + ch + 1],
                          scale=PAR[:, l, PC["conv_ln_g"] + ch:PC["conv_ln_g"] + ch + 1])

            for tb in range(NTB):
                conv_tb(0, tb)
                conv_tb(1, tb)
                if tb > 0:
                    ln_tb(tb - 1)
            ln_tb(NTB - 1)
            preissue(w_in[l], [512, 768, 640])
            barrier()

            BXP = FS[:, 0, :]
            P.memset("dve", FS[:, 0, 0:32], 0.0, ["BXP"])
            T1 = FS[:, 0, 32:32 + S]
            U = FS[:, 1, 32:32 + S]
            T2 = FS[:, 2, 32:32 + S]
            T3 = FS[:, 3, 32:32 + S]
            UB = BS[:, 0, :]
            BG = BS[:, 1, :]
            lam = PAR[:, l, PC["lru_lambda"]:PC["lru_lambda"] + 2]
            P.act(CST[:, 4:6], lam, AF.Exp, ["PAR"], ["CL"], scale=-1.0)
            P.act(CST[:, 4:6], CST[:, 4:6], AF.Ln, ["CL", "CST"], ["CL"], bias=ONEAP, scale=1.0)
            P.ts("dve", CST[:, 6:8], CST[:, 4:6], -16.0, None, ALU.mult, ALU.bypass, ["CL"], ["CL2"])
            P.ts("dve", CST[:, 4:6], CST[:, 4:6], -8.0, None, ALU.mult, ALU.bypass, ["CL", "CL2"], ["CL"])
            lw0 = PC["lru_conv_w"]
            for ch in range(2):
                wb, wk = load_w(w_in[l], 512 + 128 * ch, 128)
                wb2, wk2 = load_w(w_in[l], 768 + 128 * ch, 128)
                for tb in range(NTB):
                    ps, pk = psum()
                    proj(ps[:, :], pk, wb, wk, 0, 128, Hrhs(tb))
                    P.copy("act", BXP[:, 32 + tb * TB:32 + (tb + 1) * TB], ps[:, :], [pk], ["BXP"])
                    ps2, pk2 = psum()
                    proj(ps2[:, :], pk2, wb2, wk2, 0, 128, Hrhs(tb))
                    P.act(BG[:, tbs(tb)], ps2[:, :], AF.Gelu_apprx_tanh, [pk2], ["BG"])
                P.ts("dve", U, BXP[:, 29:29 + S], PAR[:, l, lw0 + 4 * ch:lw0 + 4 * ch + 1],
                     PAR[:, l, PC["lru_conv_b"] + ch:PC["lru_conv_b"] + ch + 1], ALU.mult, ALU.add, ["BXP", "PAR"], ["U"])
                for j in range(1, 4):
                    P.stt("dve", U, BXP[:, 29 + j:29 + j + S], PAR[:, l, lw0 + 4 * ch + j:lw0 + 4 * ch + j + 1], U,
                          ALU.mult, ALU.add, ["BXP", "PAR", "U"], ["U"])
                P.copy("act", UB, U, ["U"], ["UB"])
                for tb in range(NTB):
                    ps, pk = psum()
                    P.mm(ps[:, :], GW[:, ch, :], UB[:, tbs(tb)], True, True, ["GW", "UB"], [pk])
                    P.act(T1[:, tbs(tb)], ps[:, :], AF.Sigmoid, [pk, "PAR"], ["BXP"],
                          bias=PAR[:, l, PC["lru_b_a"] + ch:PC["lru_b_a"] + ch + 1], scale=1.0)
                    ps2, pk2 = psum()
                    P.mm(ps2[:, :], GW[:, 2 + ch, :], UB[:, tbs(tb)], True, True, ["GW", "UB"], [pk2])
                    P.act(T3[:, tbs(tb)], ps2[:, :], AF.Sigmoid, [pk2, "PAR"], ["T3"],
                          bias=PAR[:, l, PC["lru_b_i"] + ch:PC["lru_b_i"] + ch + 1], scale=1.0)
                P.act(T2, T1, AF.Exp, ["BXP", "CL"], ["T2"], scale=CST[:, 4 + ch:5 + ch])
                P.act(T1, T1, AF.Exp, ["BXP", "CL2"], ["BXP"], scale=CST[:, 6 + ch:7 + ch])
                P.act(T1, T1, AF.Sqrt, ["BXP", "CST"], ["BXP"], bias=ONEAP, scale=-1.0)
                P.tt("dve", T3, T3, U, ALU.mult, ["T3", "U"], ["T3"])
                P.tt("dve", T3, T3, T1, ALU.mult, ["T3", "BXP"], ["T3"])
                P.scan(T1, T2, T3, ["T2", "T3"], ["BXP"])
                P.tt("dve", CAT[:, 2 + ch, :], T1, BG, ALU.mult, ["BXP", "BG"], [CK(2 + ch, tb) for tb in range(NTB)])
            preissue(w_out[l], [0, 128, 256])
            barrier()

            YT = FS.rearrange("p s t -> p (s t)")[:, 0:16 * TB].rearrange("p (c q t) -> p c q t", c=8, q=2)
            proj_out(l, w_out[l], lambda kc, tb: (CAT[:, kc, tbs(tb)], CK(kc, tb)), "g_mix_post", YT,
                     next_pre=("g_mem_pre", lambda c, tb: (H[:, c, tbs(tb)], HK(c, tb)), [0, 1, 2]))
            pre_norm(l, "g_mem_pre", [3])
            preissue(w_mq[l], [0, 128])
            barrier()
            if ("x1_%d" % l) in tap_out:
                for c in range(8):
                    P.dma("sp", tap_out["x1_%d" % l][c * 128:(c + 1) * 128, :], X[:, c, :], [XK(c, tb) for tb in range(NTB)], [], "out")
                barrier()

            QM = z_bf(0, 8 * S).rearrange("p (c t) -> p c t", c=8)
            MT = z_f32(32768, 8 * NMEM).rearrange("p (c t) -> p c t", c=8)
            MN = z_bf(32768 + 8192, 8 * NMEM).rearrange("p (c t) -> p c t", c=8)
            KM = z_bf(32768 + 8192 + 4096, 8 * NMEM).rearrange("p (c t) -> p c t", c=8)
            VM = z_bf(32768 + 8192 + 8192, 2 * D).rearrange("p (m n) -> p m n", m=2)
            PM = z_bf(32768 + 8192 + 12288, 4 * TB).rearrange("p (a t) -> p a t", a=4)
            DN = z_f32(32768 + 8192 + 16384, 2 * TB).rearrange("p (a t) -> p a t", a=2)
            for c in range(8):
                P.dma("sp", MT[:, c, :], memT[c * 128:(c + 1) * 128, :], [], ["MT"], "mt")

            def comp_q(oc, wbk):
                for tb in range(NTB):
                    ps, pk = psum()
                    proj(ps[:, :], pk, wbk[0], wbk[1], 0, 128, Hrhs(tb))
                    P.copy("act", QM[:, oc, tbs(tb)], ps[:, :], [pk], ["QM%d.%d" % (oc, tb)])

            stream([0, 1], lambda oc: load_w(w_mq[l], 128 * oc), comp_q)
            ps, pk = psum()
            for c in range(8):
                si = st["sq"] % 4
                st["sq"] += 1
                P.act(SQ[:, si, 0:NMEM], MT[:, c, :], AF.Square, ["MT"], ["SQ%d" % si])
                P.mm(ps[:, 0:NMEM], ONES, SQ[:, si, 0:NMEM], c == 0, c == 7, ["SQ%d" % si, "ONES"], [pk])
            ri = st["rs"] % 2
            st["rs"] += 1
            P.act(RS[:, ri, 0:NMEM], ps[:, 0:NMEM], AF.Ln, [pk, "CST"], ["RS%d" % ri], bias=EPSAP, scale=1.0 / D)
            P.act(RS[:, ri, 0:NMEM], RS[:, ri, 0:NMEM], AF.Exp, ["RS%d" % ri], ["RS%d" % ri], scale=-0.5)
            gk0 = PC["g_mem_kv"]
            for c in range(8):
                P.stt("dve", MN[:, c, :], MT[:, c, :], PAR[:, l, gk0 + c:gk0 + c + 1], RS[:, ri, 0:NMEM], ALU.mult, ALU.mult,
                      ["MT", "PAR", "RS%d" % ri], ["MN%d" % c])
            stream([2, 3], lambda oc: load_w(w_mq[l], 128 * oc), comp_q)

            def comp_k(oc, wbk):
                ps, pk = psum()
                proj(ps[:, 0:NMEM], pk, wbk[0], wbk[1], 0, 128, lambda kc: (MN[:, kc, :], "MN%d" % kc))
                P.copy("act", KM[:, oc, :], ps[:, 0:NMEM], [pk], ["KM%d" % oc])

            stream(list(range(8)), lambda oc: load_w(w_mkv[l], 128 * oc), comp_k)

            def comp_v(n, wbk):
                ps, pk = psum()
                for mt in range(2):
                    for kc in range(8):
                        P.mm(ps[:, 128 * mt:128 * mt + 128], MN[:, kc, 128 * mt:128 * mt + 128], WS[:, wbk[0], kc, :], kc == 0, kc == 7,
                             ["MN%d" % kc, wbk[1]], [pk])
                psv = ps[:, 0:256].rearrange("p (m c) -> p m c", m=2)
                P.copy("act", VM[:, :, 128 * n:128 * n + 128], psv, [pk], ["VM"])

            stream(list(range(8)), lambda n: load_w(w_mkv[l], D + 128 * n), comp_v)
            stream([4, 5, 6, 7], lambda oc: load_w(w_mq[l], 128 * oc), comp_q)
            cm = {"pm": 0}

            def cross_unit(hd, tb):
                info = {}

                def a():
                    pks = []
                    for mt in range(2):
                        ps, pk = psum()
                        for dc in range(2):
                            P.mm(ps[:, :], KM[:, 2 * hd + dc, 128 * mt:128 * mt + 128], QM[:, 2 * hd + dc, tbs(tb)], dc == 0, dc == 1,
                                 ["KM%d" % (2 * hd + dc), "QM%d.%d" % (2 * hd + dc, tb)], [pk])
                        pi = cm["pm"] % 4
                        cm["pm"] += 1
                        P.act(PM[:, pi, :], ps[:, :], AF.Exp, [pk], ["PM%d" % pi], scale=1.0 / 16.0)
                        pks.append(pi)
                    info["pks"] = pks

                def b():
                    pks = info["pks"]
                    psd, pdk = psum()
                    for mt in range(2):
                        P.mm(psd[:, :], ONES, PM[:, pks[mt], :], mt == 0, mt == 1, ["ONES", "PM%d" % pks[mt]], [pdk])
                    di = (hd * NTB + tb) % 2
                    P.act(DN[:, di, :], psd[:, :], AF.Ln, [pdk], ["DN%d" % di])
                    P.act(DN[:, di, :], DN[:, di, :], AF.Exp, ["DN%d" % di], ["DN%d" % di], scale=-1.0)
                    for dc in range(2):
                        pso, pok = psum()
                        for mt in range(2):
                            P.mm(pso[:, :], VM[:, mt, 256 * hd + 128 * dc:256 * hd + 128 * dc + 128], PM[:, pks[mt], :], mt == 0, mt == 1,
                                 ["VM", "PM%d" % pks[mt]], [pok])
                        P.tt("dve", QM[:, 2 * hd + dc, tbs(tb)], pso[:, :], DN[:, di, :], ALU.mult, [pok, "DN%d" % di],
                             ["QM%d.%d" % (2 * hd + dc, tb)])
                return a, b

            cunits = [cross_unit(hd, tb) for hd in range(4) for tb in range(NTB)]
            pend_b = None
            for (a_fn, b_fn) in cunits:
                a_fn()
                if pend_b is not None:
                    pend_b()
                pend_b = b_fn
            pend_b()
            preissue(w_mo[l], [0, 128, 256])
            barrier()
            YT2 = z_f32(32768, 16 * TB).rearrange("p (c q t) -> p c q t", c=8, q=2)
            HT = 1024
            Hflat = H.rearrange("p c t -> p (c t)")
            HF = Hflat[:, 0:8 * HT].rearrange("p (c t) -> p c t", c=8)
            proj_out(l, w_mo[l], lambda kc, tb: (QM[:, kc, tbs(tb)], "QM%d.%d" % (kc, tb)), "g_mem_post", YT2,
                     next_pre=("g_ffn_pre", lambda c, tb: (HF[:, c, (tb % 2) * TB:(tb % 2 + 1) * TB], "HF%d.%d" % (c, tb % 2)),
                               [0, 1]))
            preissue(w_up[l], [0, DFF, 128])
            barrier()
            if ("x2_%d" % l) in tap_out:
                for c in range(8):
                    P.dma("sp", tap_out["x2_%d" % l][c * 128:(c + 1) * 128, :], X[:, c, :], [XK(c, tb) for tb in range(NTB)], [], "out")
                barrier()

            YFB = Hflat[:, 8 * HT:16 * HT].bitcast(F32).rearrange("p (c t) -> p c t", c=8)
            G = z_bf(0, NGC * HT).rearrange("p (c t) -> p c t", c=NGC)
            WD = z_bf(45056, 2 * NGC * 128).rearrange("p (b c n) -> p b c n", b=2, c=NGC)
            YP = CSNRAW[:, 0:4 * (TB + 2)].rearrange("p (a b t) -> p a b t", a=2, b=2)
            UG = z_f32(45056 + 11264, 2 * 2 * TB).rearrange("p (a b t) -> p a b t", a=2, b=2)
            YFA = z_f32(45056 + 11264 + 8192, 8 * TB).rearrange("p (c t) -> p c t", c=8)
            HALO = z_f32(45056 + 11264 + 8192 + 16384, 2 * NGC * 2).rearrange("p (c t) -> p c t", c=2 * NGC)
            YFv = [YFA, YFB]
            SQF = SQ.rearrange("p a t -> p (a t)").bitcast(F32).rearrange("p (a t) -> p a t", a=2)
            fw0 = PC["ffn_conv_w"]
            fb0 = PC["ffn_conv_b"]
            for half in range(2):
                tasks = [(gc, which) for gc in range(NGC) for which in range(2)]

                def comp_up(t, wbk):
                    gc, which = t
                    b_, wk = wbk
                    cc = gc + NGC * which
                    ypk = "YP%d" % which
                    for tq in range(2):
                        ps, pk = psum()
                        proj(ps[:, :], pk, b_, wk, 0, 128, lambda kc: (HF[:, kc, tq * TB:(tq + 1) * TB], "HF%d.%d" % (kc, tq)))
                        hkey = "YH%d.%d" % (which, tq)
                        yk = ypk + ".%d" % tq
                        P.copy("act", YP[:, which, tq, 2:2 + TB], ps[:, :], [pk], [yk])
                        if tq == 0 and half == 0:
                            pass
                        elif tq == 0:
                            P.copy("act", YP[:, which, 0, 0:2], HALO[:, cc, :], ["HALO%d" % cc], [hkey])
                        else:
                            P.copy("act", YP[:, which, 1, 0:2], YP[:, which, 0, TB:TB + 2], [ypk + ".0"], [hkey])
                            if half == 0:
                                P.copy("act", HALO[:, cc, :], YP[:, which, 1, TB:TB + 2], [yk], ["HALO%d" % cc])
                        if which == 0 and gc % 2 == 1:
                            ug = SQF[:, tq, :]
                            ugk = ["SQ%d" % (2 * tq), "SQ%d" % (2 * tq + 1)]
                        else:
                            ug = UG[:, which, tq, :]
                            ugk = ["UG%d.%d" % (which, tq)]
                        P.act(ug, ps[:, :], AF.Identity, [pk, "PAR"], ugk,
                              bias=PAR[:, l, fb0 + cc:fb0 + cc + 1], scale=PAR[:, l, fw0 + 3 * cc + 2:fw0 + 3 * cc + 3])
                        P.stt("dve", ug, YP[:, which, tq, 0:TB], PAR[:, l, fw0 + 3 * cc:fw0 + 3 * cc + 1], ug,
                              ALU.mult, ALU.add, [yk, hkey, "PAR"] + ugk, ugk)
                        P.stt("dve", ug, YP[:, which, tq, 1:1 + TB], PAR[:, l, fw0 + 3 * cc + 1:fw0 + 3 * cc + 2], ug,
                              ALU.mult, ALU.add, [yk, hkey, "PAR"] + ugk, ugk)
                        def s2(which=which, ug=ug, ugk=ugk, gc=gc, tq=tq):
                            if which == 0:
                                P.act(ug, ug, AF.Gelu_apprx_tanh, ugk, ugk)
                            else:
                                if gc % 2 == 1:
                                    gate, gk = SQF[:, tq, :], ["SQ%d" % (2 * tq), "SQ%d" % (2 * tq + 1)]
                                else:
                                    gate, gk = UG[:, 0, tq, :], ["UG0.%d" % tq]
                                P.tt("dve", G[:, gc, tq * TB:(tq + 1) * TB], gate, UG[:, 1, tq, :], ALU.mult,
                                     gk + ["UG1.%d" % tq], ["G%d.%d" % (gc, tq)])
                        gi["n"] += 1
                        defer.append((gi["n"] + (2 if which == 0 else 1), s2))
                        while defer and defer[0][0] <= gi["n"]:
                            defer.pop(0)[1]()

                defer = []
                gi = {"n": 0}
                if half == 0:
                    P.memset("dve", YP[:, 0, 0, 0:2], 0.0, ["YH0.0"])
                    P.memset("dve", YP[:, 1, 0, 0:2], 0.0, ["YH1.0"])
                stream(tasks, lambda t: load_w(w_up[l], DFF * t[1] + 128 * t[0]), comp_up)
                while defer:
                    defer.pop(0)[1]()
                dst_ = {"i": 0}

                def load_d(oc):
                    db = dst_["i"] % 2
                    dst_["i"] += 1
                    dk = "WD%d" % db
                    P.dma("pool", WD[:, db, :, :], w_dn[l][oc], [], [dk], "wd%d" % db)
                    return db, dk

                hooks_pre, hooks_post = {}, {}
                if half == 0:
                    hf_dst = lambda c, tb: (HF[:, c, (tb % 2) * TB:(tb % 2 + 1) * TB], "HF%d.%d" % (c, tb % 2))
                    pa = pre_norm_parts(l, "g_ffn_pre", 2, hf_dst)
                    pb_ = pre_norm_parts(l, "g_ffn_pre", 3, hf_dst)
                    hooks_pre[1] = [pa[0]]
                    hooks_post[1] = [pa[1], pa[2]]
                    hooks_post[2] = [pa[3], pb_[0]]
                    hooks_post[3] = [pb_[1], pb_[2]]
                    hooks_post[4] = [pb_[3]]

                def comp_d(oc, wbk):
                    db, dk = wbk
                    for f in hooks_pre.get(oc, ()):
                        f()
                    for tq in range(2):
                        ps, pk = psum()
                        for gc in range(NGC):
                            P.mm(ps[:, :], WD[:, db, gc, :], G[:, gc, tq * TB:(tq + 1) * TB], gc == 0, gc == NGC - 1,
                                 [dk, "G%d.%d" % (gc, tq)], [pk])
                        P.copy("act", YFv[tq][:, oc, :], ps[:, :], [pk], ["YF%d.%d" % (oc, tq)])
                    for f in hooks_post.get(oc, ()):
                        f()

                stream(list(range(8)), load_d, comp_d, ahead=1)
                for tq in range(2):
                    post_norm_tb(l, "g_ffn_post", 2 * half + tq, lambda c: (YFv[tq][:, c, :], "YF%d.%d" % (c, tq)))
            if l + 1 < nl:
                preissue(w_in[l + 1], [1024, 1280])
            barrier()
            if ("x3_%d" % l) in tap_out:
                for c in range(8):
                    P.dma("sp", tap_out["x3_%d" % l][c * 128:(c + 1) * 128, :], X[:, c, :], [XK(c, tb) for tb in range(NTB)], [], "out")
                barrier()

        for c in range(8):
            P.dma("sp", outT[c * 128:(c + 1) * 128, :], X[:, c, :], [XK(c, tb) for tb in range(NTB)], [], "out")

        with nc.Block() as block:
            P.emit(nc, block)
        P._stack.close()
    return nc


def _host_tables():
    inv = (1.0 / (np.float32(10000.0) ** (np.arange(0, 64, 2, dtype=np.float32) / np.float32(64)))).astype(np.float32)
    ang = (np.arange(S, dtype=np.float32)[:, None] * inv[None, :]).astype(np.float32)
    cos = np.cos(ang).astype(np.float32)
    sin = np.sin(ang).astype(np.float32)
    d = np.arange(128) % 64
    cosT = np.ascontiguousarray(cos[:, d % 32].T)
    sgn = np.where(d < 32, -1.0, 1.0).astype(np.float32)
    sinT = np.ascontiguousarray((sin[:, d % 32] * sgn[None, :]).T)
    k = np.arange(128)[:, None]
    q = np.arange(128)[None, :]
    m_same = (q >= k).astype(np.float32)
    m_next = (q <= k).astype(np.float32)
    msk = np.concatenate([m_same, m_next, m_same, m_next, m_same, m_same, m_same, m_same], axis=1)
    return cosT, sinT, np.ascontiguousarray(msk)


def _perm_matrix():
    pm = np.zeros((128, 128), np.float32)
    for m in range(128):
        pm[64 * (m // 64) + ((m % 64) + 32) % 64, m] = 1.0
    return pm


def _prep_shared(inp):
    f = lambda a: np.ascontiguousarray(np.asarray(a, dtype=np.float32))
    w_in = f(inp["w_in"])
    cols = list(range(1024))
    for p in range(4):
        qc = [1024 + 128 * p + i for i in range(128)]
        kc = [1536 + 128 * p + i for i in range(128)]
        sw = [64 * (i // 64) + ((i % 64) + 32) % 64 for i in range(128)]
        cols += qc + [qc[j] for j in sw] + kc + [kc[j] for j in sw]
    cols += list(range(2048, 2560))
    w_in_ext = np.ascontiguousarray(w_in[:, :, cols])
    assert w_in_ext.shape[2] == W_IN_EXT
    par = np.zeros((NL, 128, NPAR), np.float32)

    def vec(name, v, nch):
        par[:, :, PC[name]:PC[name] + nch] = f(v).reshape(NL, nch, 128).transpose(0, 2, 1)

    for n in ("g_mix_pre", "g_mix_post", "g_mem_pre", "g_mem_kv", "g_mem_post", "g_ffn_pre", "g_ffn_post"):
        vec(n, inp[n], 8)
    for n in ("conv_b", "conv_ln_g", "conv_ln_b", "lru_conv_b", "lru_b_a", "lru_b_i", "lru_lambda"):
        vec(n, inp[n], 2)
    vec("ffn_conv_b", inp["ffn_conv_b"], 44)
    cw = f(inp["conv_w"]).reshape(NL, 31, 2, 128).transpose(0, 3, 2, 1).reshape(NL, 128, 62)
    par[:, :, PC["conv_w"]:PC["conv_w"] + 62] = cw
    lw = f(inp["lru_conv_w"]).reshape(NL, 4, 2, 128).transpose(0, 3, 2, 1).reshape(NL, 128, 8)
    par[:, :, PC["lru_conv_w"]:PC["lru_conv_w"] + 8] = lw
    fw = f(inp["ffn_conv_w"]).reshape(NL, 3, 44, 128).transpose(0, 3, 2, 1).reshape(NL, 128, 132)
    par[:, :, PC["ffn_conv_w"]:PC["ffn_conv_w"] + 132] = fw
    gw = np.zeros((NL, 4, 128, 128), np.float32)
    wa = f(inp["lru_w_a"])
    wi = f(inp["lru_w_i"])
    for ch in range(2):
        for hh in range(2):
            gw[:, ch, 64 * hh:64 * hh + 64, 64 * hh:64 * hh + 64] = wa[:, 2 * ch + hh]
            gw[:, 2 + ch, 64 * hh:64 * hh + 64, 64 * hh:64 * hh + 64] = wi[:, 2 * ch + hh]
    cosT, sinT, msk = _host_tables()
    def tile_w(w):
        L_, K_, N_ = w.shape
        return np.ascontiguousarray(w.reshape(L_, K_ // 128, 128, N_ // 128, 128).transpose(0, 3, 2, 1, 4))

    return {
        "par": par, "w_in": tile_w(w_in_ext), "gw": gw, "w_out": tile_w(f(inp["w_out"])), "w_mq": tile_w(f(inp["w_mem_q"])),
        "w_mkv": tile_w(f(inp["w_mem_kv"])), "w_mo": tile_w(f(inp["w_mem_o"])), "w_up": tile_w(f(inp["w_up"])),
        "w_dn": tile_w(f(inp["w_down"])), "cos": cosT, "sin": sinT, "msk": msk,
        "ident": np.eye(128, dtype=np.float32),
        "perm": _perm_matrix(),
    }


def kernel(**inputs):
    shared = _prep_shared(inputs)
    x = np.asarray(inputs["x"], dtype=np.float32)
    mem = np.asarray(inputs["mem"], dtype=np.float32)
    in_maps = []
    for b in range(NCORES):
        m = dict(shared)
        m["xT"] = np.ascontiguousarray(x[b].T)
        m["memT"] = np.ascontiguousarray(mem[b].T)
        in_maps.append(m)
    nc = build()
    res = run_bass_kernel_spmd(nc, in_maps, core_ids=list(range(NCORES)))
    out = np.stack([np.ascontiguousarray(res.results[b]["outT"].T) for b in range(NCORES)], axis=0)
    return out.astype(np.float32)
```
